# Optimizing a Trainium2 kernel written in Bass

```python
import jax, jax.numpy as jnp
from jax import lax
import numpy as np

D_MODEL = 1024
BATCH = 4
SEQ = 4096
DEPTH = 1

CHUNK = 64
N_META = 16
D_A = 1024
D_B = 1024
CONV_A = 31
CONV_B = 3
EPS = 1e-6
SPLITS = (D_A, D_A, D_A, D_B, D_B, D_B, D_B, D_MODEL, D_MODEL)
D_IN = sum(SPLITS)

kernel_name = "hybrid_gated_conformer_shortconv_block"


def _rmsnorm(x, g):
    xf = x.astype(jnp.float32)
    y = xf * lax.rsqrt(jnp.mean(xf * xf, axis=-1, keepdims=True) + EPS)
    return (y * g.astype(jnp.float32)).astype(x.dtype)


def _layernorm(x, g, b):
    xf = x.astype(jnp.float32)
    mu = jnp.mean(xf, axis=-1, keepdims=True)
    var = jnp.mean(jnp.square(xf - mu), axis=-1, keepdims=True)
    y = (xf - mu) * lax.rsqrt(var + EPS)
    return (y * g.astype(jnp.float32) + b.astype(jnp.float32)).astype(x.dtype)


def _causal_dwconv(x, w):
    k = w.shape[0]
    return lax.conv_general_dilated(
        x, w.astype(x.dtype)[:, None, :], window_strides=(1,), padding=[(k - 1, 0)],
        dimension_numbers=("NWC", "WIO", "NWC"), feature_group_count=x.shape[-1])


def setup_inputs(seed: int = 0) -> dict:
    key = jax.random.key(seed)
    ks = jax.random.split(key, 16)
    f = jnp.float32
    L = DEPTH
    nrm = lambda k, shp, s: jax.random.normal(k, shp, f) * s
    return {
        "x": jax.random.normal(ks[0], (BATCH, SEQ, D_MODEL), f),
        "meta_tokens": nrm(ks[1], (N_META, D_MODEL), 1.0),
        "norm_g": 1.0 + nrm(ks[2], (L, D_MODEL), 0.02),
        "w_in": nrm(ks[3], (L, D_MODEL, D_IN), D_MODEL ** -0.5),
        "conv_a_w": nrm(ks[4], (L, CONV_A, D_A), CONV_A ** -0.5),
        "conv_a_b": nrm(ks[5], (L, D_A), 0.02),
        "ln_a_g": 1.0 + nrm(ks[6], (L, D_A), 0.02),
        "ln_a_b": nrm(ks[7], (L, D_A), 0.02),
        "w_a_out": nrm(ks[8], (L, D_A, D_MODEL), D_A ** -0.5),
        "b_a_out": nrm(ks[9], (L, D_MODEL), 0.02),
        "conv_b_w": nrm(ks[10], (L, CONV_B, D_B), CONV_B ** -0.5),
        "w_b_out": nrm(ks[11], (L, D_B, D_MODEL), D_B ** -0.5),
        "w_out": nrm(ks[12], (L, D_MODEL, D_MODEL), D_MODEL ** -0.5),
        "final_g": 1.0 + nrm(ks[13], (D_MODEL,), 0.02),
    }


def reference(x, meta_tokens, norm_g, w_in, conv_a_w, conv_a_b, ln_a_g, ln_a_b,
              w_a_out, b_a_out, conv_b_w, w_b_out, w_out, final_g):
    bsz = x.shape[0]
    meta = jnp.broadcast_to(meta_tokens.astype(x.dtype)[None], (bsz, N_META, D_MODEL))
    s = jnp.concatenate([meta, x], axis=1)
    idx = np.cumsum(SPLITS)[:-1].tolist()
    for l in range(DEPTH):
        h = _rmsnorm(s, norm_g[l])
        proj = jnp.einsum("bld,de->ble", h, w_in[l])
        a_val, a_glu, a_z, b_B, b_C, b_x, b_z, g_a, g_b = jnp.split(proj, idx, axis=-1)
        ua = a_val * jax.nn.sigmoid(a_glu)
        ua = _causal_dwconv(ua, conv_a_w[l]) + conv_a_b[l]
        ua = jax.nn.silu(_layernorm(ua, ln_a_g[l], ln_a_b[l]))
        ya = jnp.einsum("blc,cd->bld", ua * jax.nn.silu(a_z), w_a_out[l]) + b_a_out[l]
        ub = b_B * _causal_dwconv(b_C * b_x, conv_b_w[l])
        yb = jnp.einsum("blc,cd->bld", ub * jax.nn.silu(b_z), w_b_out[l])
        m = jax.nn.sigmoid(g_a) * ya + jax.nn.sigmoid(g_b) * yb
        s = s + jnp.einsum("bld,de->ble", m, w_out[l])
    y = _rmsnorm(s, final_g)
    return y[:, N_META:, :]
```

```python
import numpy as np
from contextlib import ExitStack

import concourse.bass as bass
import concourse.mybir as mybir
from concourse.bass_utils import run_bass_kernel_spmd

F32 = mybir.dt.float32
BF16 = mybir.dt.bfloat16
AF = mybir.ActivationFunctionType
ALU = mybir.AluOpType

D = 1024
T = 2048
H = 32
TT = T + H
NJ = 8
KA = 31
KB = 3
EPS = 1e-6
NCV = 38
NWT = 6


class SemSrc:
    def __init__(self, nc, ctx, name, step):
        self.sem = ctx.enter_context(nc.semaphore(name))
        self.count = 0
        self.step = step
        self.name = name

    def mark(self, ins):
        ins.then_inc(self.sem, self.step)
        self.count += self.step
        return (self, self.count)


class Eng(SemSrc):
    def __init__(self, nc, ctx, e, name, kind):
        super().__init__(nc, ctx, "s_" + name, 1)
        self.e = e
        self.kind = kind
        self.waited = {}

    def wait(self, ev):
        src, val = ev
        if src is self:
            if self.kind == "pe":
                return
        if self.waited.get(src, 0) >= val:
            return
        self.e.wait_ge(src.sem, val)
        self.waited[src] = val


class Buf:
    def __init__(self, name=""):
        self.name = name
        self.w = None
        self.r = {}


def _deps(reads, writes):
    evs = []
    for b in reads:
        if b.w is not None:
            evs.append(b.w)
    for b in writes:
        if b.w is not None:
            evs.append(b.w)
        evs += list(b.r.items())
    return evs


def _commit(ev, reads, writes):
    for b in reads:
        b.r[ev[0]] = max(b.r.get(ev[0], 0), ev[1])
    for b in writes:
        b.w = ev
        b.r = {}


def op(eng, fn, reads=(), writes=()):
    for ev in _deps(reads, writes):
        eng.wait(ev)
    ins = fn()
    ev = eng.mark(ins)
    _commit(ev, reads, writes)
    return ev


def dma(qeng, dsem, fn, reads=(), writes=()):
    for ev in _deps(reads, writes):
        qeng.wait(ev)
    ins = fn()
    ev = dsem.mark(ins)
    _commit(ev, reads, writes)
    return ev


NP_TAPS = [14, 14, 14, 14, 14, 14, 18, 24]
NQ_TAPS = [0, 0, 0, 0, 0, 0, 0, 0]
NPMAX = max(NP_TAPS)
DVE_RATE = 1.2
POOL_RATE = 0.8


def build_nc():
    nc = bass.Bass("TRN2", target_bir_lowering=False)
    xm = nc.dram_tensor("xm", [T, D], F32, kind="ExternalInput").ap()
    xh = nc.dram_tensor("xh", [H, D], F32, kind="ExternalInput").ap()
    w_in = nc.dram_tensor("w_in", [D, 9 * D], F32, kind="ExternalInput").ap()
    w_a = nc.dram_tensor("w_a", [D, D], F32, kind="ExternalInput").ap()
    w_b = nc.dram_tensor("w_b", [D, D], F32, kind="ExternalInput").ap()
    w_o = nc.dram_tensor("w_o", [D, D], F32, kind="ExternalInput").ap()
    cvec_d = nc.dram_tensor("cvec", [128, NJ * NCV], F32, kind="ExternalInput").ap()
    g1_d = nc.dram_tensor("g1", [128, D], F32, kind="ExternalInput").ap()
    g2_d = nc.dram_tensor("g2", [128, D], F32, kind="ExternalInput").ap()
    id_d = nc.dram_tensor("ident", [128, 128], F32, kind="ExternalInput").ap()
    y = nc.dram_tensor("y", [T, D], F32, kind="ExternalOutput").ap()

    w_in_v = w_in.rearrange("(k p) c -> p k c", p=128)
    w_a_v = w_a.rearrange("(k p) c -> p k c", p=128)
    w_b_v = w_b.rearrange("(k p) c -> p k c", p=128)
    w_o_v = w_o.rearrange("(k p) c -> p k c", p=128)

    with ExitStack() as ctx:
        PE = Eng(nc, ctx, nc.tensor, "pe", "pe")
        ACT = Eng(nc, ctx, nc.scalar, "act", "act")
        DVE = Eng(nc, ctx, nc.vector, "dve", "dve")
        POOL = Eng(nc, ctx, nc.gpsimd, "pool", "pool")
        SP = Eng(nc, ctx, nc.sync, "sp", "sp")
        engines = [PE, ACT, DVE, POOL, SP]
        dsems = []

        def new_dsem(name):
            s = SemSrc(nc, ctx, name, 16)
            dsems.append(s)
            return s

        def barrier():
            srcs = engines + dsems
            for e in engines:
                for s in srcs:
                    if s is e or s.count == 0:
                        continue
                    e.wait((s, s.count))

        def sb(c, name, shape, dt):
            return c.enter_context(nc.sbuf_tensor(name, shape, dt))

        hT = sb(ctx, "hT", [128, NJ, TT], BF16)
        UC = sb(ctx, "UC", [128, NJ, T], F32)
        UCb = UC[:].bitcast(BF16)
        VB = sb(ctx, "VB", [128, NJ, T], BF16)
        wt = [sb(ctx, f"wt{i}", [128, NJ, 128], BF16) for i in range(NWT)]
        cvec = sb(ctx, "cvec_s", [128, NJ, NCV], F32)
        idf = sb(ctx, "idf", [128, 128], F32)
        idb = sb(ctx, "idb", [128, 128], BF16)
        onesb = sb(ctx, "onesb", [128, 2], BF16)
        onesf = sb(ctx, "onesf", [128, 128], F32)
        ss1 = sb(ctx, "ss1", [128, 32], F32)
        rs1 = sb(ctx, "rs1", [128, 32], F32)
        st = sb(ctx, "st", [128, 32], F32)
        stw = sb(ctx, "stw", [128, 6, 16], F32)
        epsc = sb(ctx, "epsc", [128, 1], F32)

        NRING = 4
        ring = [ctx.enter_context(nc.psum_tensor(f"ring{i}", [128, 1024], F32)) for i in range(NRING)]
        ringB = [Buf(f"ring{i}") for i in range(NRING)]
        statsB = Buf("stats")
        ucnt = [0]

        def next_slot():
            s = ucnt[0] % NRING
            ucnt[0] += 1
            return ring[s], ringB[s]

        wtB = [Buf(f"wt{i}") for i in range(NWT)]
        wsem = [new_dsem(f"wsem{i}") for i in range(NWT)]
        csem = new_dsem("csem")
        hTB = Buf("hT")
        constB = Buf("const")

        tiles = []

        def G(g, j):
            return (w_in_v, g * D + j * 128)

        LEAD = 1
        for step in range(NJ + LEAD):
            if step < NJ:
                tiles += [G(1, step), G(0, step)]
            if step >= LEAD:
                j = step - LEAD
                tiles += [G(4, j), G(5, j), G(3, j), G(6, j)]
        for j in range(NJ):
            tiles += [G(2, j)]
        for i in range(NJ):
            tiles += [G(7, i), (w_a_v, i * 128), G(8, i), (w_b_v, i * 128)]
        tstate = {"issued": 0, "next_use": 0, "slot_users_left": [0] * NWT}

        def issue_tiles():
            while tstate["issued"] < len(tiles):
                i = tstate["issued"]
                s = i % NWT
                if tstate["slot_users_left"][s] != 0:
                    break
                view, c0 = tiles[i]
                dma(POOL, wsem[s],
                    lambda: nc.gpsimd.dma_start(out=wt[s][:], in_=view[:, :, c0:c0 + 128]),
                    writes=[wtB[s]])
                tstate["slot_users_left"][s] = -1
                tstate["issued"] += 1

        def acquire_tile(nusers):
            i = tstate["next_use"]
            tstate["next_use"] += 1
            assert i < tstate["issued"], "tile not issued"
            s = i % NWT
            tstate["slot_users_left"][s] = nusers
            return s

        def release_tile(s):
            tstate["slot_users_left"][s] -= 1
            if tstate["slot_users_left"][s] == 0:
                issue_tiles()

        c4 = new_dsem("c4")
        idB = Buf()
        dma(SP, csem, lambda: nc.sync.dma_start(out=cvec[:], in_=cvec_d.rearrange("p (j r) -> p j r", j=NJ)), writes=[constB])
        dma(SP, c4, lambda: nc.sync.dma_start(out=idf[:], in_=id_d), writes=[idB])
        issue_tiles()
        op(DVE, lambda: nc.vector.tensor_copy(idb[:], idf[:]), reads=[idB], writes=[constB])
        op(DVE, lambda: nc.vector.memset(onesb[:], 1.0), writes=[constB])
        op(DVE, lambda: nc.vector.memset(onesf[:], 1.0), writes=[constB])
        op(DVE, lambda: nc.vector.memset(ss1[:], 0.0), writes=[constB])
        op(DVE, lambda: nc.vector.memset(epsc[:], EPS), writes=[constB])
        op(DVE, lambda: nc.vector.memset(st[:], 0.0), writes=[statsB])

        with ExitStack() as pc:
            g1 = sb(pc, "g1_s", [128, D], F32)
            g1B = Buf()
            c2 = new_dsem("c2")
            dma(SP, c2, lambda: nc.sync.dma_start(out=g1[:], in_=g1_d), writes=[g1B])
            NXT = 6
            NHN = 4
            xt = [sb(pc, f"xt{i}", [128, D], F32) for i in range(NXT)]
            xtB = [Buf() for _ in range(NXT)]
            xsem = [new_dsem(f"xsem{i}") for i in range(NXT)]
            junk = sb(pc, "junk", [128, D], BF16)
            junkB = Buf()
            hn = [sb(pc, f"hn{i}", [128, D], BF16) for i in range(NHN)]
            hnB = [Buf() for _ in range(NHN)]

            def rows_of(c):
                return H if c == 0 else 128

            def stage_a(c):
                s3 = c % NXT
                sh = c % NHN
                rows = rows_of(c)
                src = xh if c == 0 else xm[(c - 1) * 128:c * 128, :]
                dma(SP, xsem[s3], lambda: nc.sync.dma_start(out=xt[s3][0:rows, :], in_=src), writes=[xtB[s3]])
                op(ACT, lambda: nc.scalar.activation(junk[0:rows, :], xt[s3][0:rows, :], AF.Square,
                                                     accum_out=ss1[0:rows, c:c + 1]),
                   reads=[xtB[s3], constB], writes=[junkB])
                rb = Buf()
                op(ACT, lambda: nc.scalar.activation(rs1[0:rows, c:c + 1], ss1[0:rows, c:c + 1], AF.Sqrt, bias=epsc[0:rows, :], scale=1.0 / D),
                   reads=[junkB, constB], writes=[rb])
                op(DVE, lambda: nc.vector.reciprocal(rs1[0:rows, c:c + 1], rs1[0:rows, c:c + 1]),
                   reads=[rb], writes=[rb])
                op(DVE, lambda: nc.vector.scalar_tensor_tensor(hn[sh][0:rows, :], xt[s3][0:rows, :], rs1[0:rows, c:c + 1],
                                                               g1[0:rows, :], ALU.mult, ALU.mult),
                   reads=[rb, xtB[s3], g1B], writes=[hnB[sh]])

            def stage_b(c):
                s3 = c % NHN
                rows = rows_of(c)
                tok0 = 0 if c == 0 else H + (c - 1) * 128
                ps, psB = next_slot()
                psb = ps[:].bitcast(BF16)

                def emit_tr():
                    last = None
                    for k in range(NJ):
                        last = nc.tensor.transpose(psb[:, k * 128:k * 128 + rows], hn[s3][0:rows, k * 128:(k + 1) * 128],
                                                   idb[0:rows, 0:rows])
                    return last
                op(PE, emit_tr, reads=[hnB[s3], constB], writes=[psB])
                src_ps = psb[:, 0:1024].rearrange("p (k t) -> p k t", k=NJ)[:, :, 0:rows]
                op(DVE, lambda: nc.vector.tensor_copy(hT[:, :, tok0:tok0 + rows], src_ps), reads=[psB], writes=[hTB])

            AHEAD = 3
            for c in range(AHEAD):
                stage_a(c)
            for c in range(17):
                if c + AHEAD < 17:
                    stage_a(c + AHEAD)
                stage_b(c)
            barrier()

        def main_unit(s, X, XB, tok0, extra_reads=()):
            ps, psB = next_slot()

            def emit():
                last = None
                for b in range(2):
                    for k in range(NJ):
                        last = nc.tensor.matmul(ps[:, b * 512:(b + 1) * 512], wt[s][:, k, :],
                                                X(k, tok0 + b * 512, 512), start=(k == 0), stop=(k == NJ - 1))
                return last
            op(PE, emit, reads=[wtB[s], XB] + list(extra_reads), writes=[psB])
            release_tile(s)
            return ps, psB

        def halo_unit(s):
            ps, psB = next_slot()

            def emit():
                last = None
                for k in range(NJ):
                    last = nc.tensor.matmul(ps[:, 0:H], wt[s][:, k, :], hT[:, k, 0:H], start=(k == 0), stop=(k == NJ - 1))
                return last
            op(PE, emit, reads=[wtB[s], hTB], writes=[psB])
            release_tile(s)
            return ps, psB

        def XhT(k, t0, n):
            return hT[:, k, H + t0:H + t0 + n]

        def Xva(k, t0, n):
            return UCb[:, k, t0:t0 + n]

        def Xvb(k, t0, n):
            return VB[:, k, t0:t0 + n]

        def conv_unit(dgt, dgB, taps, K, src, srcB, th):
            ps, psB = next_slot()

            def emit():
                last = None
                for b in range(2):
                    t0 = H + th * 1024 + b * 512 - (K - 1)
                    for i, k in enumerate(taps):
                        last = nc.tensor.matmul(ps[:, b * 512:(b + 1) * 512], dgt[:, i, :], src[:, t0 + k:t0 + k + 512],
                                                start=(i == 0), stop=(i == len(taps) - 1))
                return last
            op(PE, emit, reads=[dgB, srcB], writes=[psB])
            return ps, psB

        ucB = [Buf(f"uc{j}") for j in range(NJ)]
        vaB = Buf("va")
        vbB = Buf("vb")

        with ExitStack() as pc:
            NUA = 3
            ua = [sb(pc, f"ua{i}", [128, TT], BF16) for i in range(NUA)]
            uaB = [Buf(f"ua{i}") for i in range(NUA)]
            sg = [sb(pc, f"sg{i}", [128, 1024], F32) for i in range(2)]
            sgB = [Buf() for _ in range(2)]
            sgh = sb(pc, "sgh", [128, H], F32)
            sghB = Buf()
            hv = sb(pc, "hv", [128, H], F32)
            hvB = Buf()
            stt = sb(pc, "stt", [128, 16], F32)
            sttB = Buf()
            dg = sb(pc, "dg", [128, NPMAX, 128], BF16)
            dgB = Buf()
            ucb = sb(pc, "ucb", [128, 1024], BF16)
            ucq = sb(pc, "ucq", [128, 1024], BF16)
            ucbB, ucqB = Buf(), Buf()
            NEV = 3
            evt = [sb(pc, f"ev{i}", [128, 1024], F32) for i in range(NEV)]
            evB = [Buf() for _ in range(NEV)]
            evc = [0]

            def next_ev():
                e = evc[0] % NEV
                evc[0] += 1
                return evt[e], evB[e]
            bc = [sb(pc, f"bc{i}", [128, 1024], F32) for i in range(2)]
            bch = sb(pc, "bch", [128, H], F32)
            sbz = [sb(pc, f"sbz{i}", [128, 1024], F32) for i in range(2)]
            cxb = sb(pc, "cxb", [128, TT], BF16)
            dg3 = sb(pc, "dg3", [128, KB, 128], BF16)
            bcB = [Buf() for _ in range(2)]
            bchB = Buf()
            sbzB = [Buf() for _ in range(2)]
            cxbB = Buf()
            dg3B = Buf()

            dve_bg = []
            pool_bg = []

            credit = [0.0, 0.0]
            lag_q = [[], []]

            def defer(fn):
                lag_q[1].append(fn)

            timed = []
            stats_ready = []

            def after_pumps(n, fn):
                timed.append([n, fn])

            def pump(force=False):
                for t_ in timed:
                    t_[0] -= 1
                due = [t_ for t_ in timed if t_[0] <= 0]
                for t_ in due:
                    timed.remove(t_)
                    t_[1]()
                for fn in lag_q[0]:
                    fn()
                lag_q[0] = lag_q[1]
                lag_q[1] = []
                credit[0] += DVE_RATE
                credit[1] += POOL_RATE
                while dve_bg and (force or credit[0] >= 1.0):
                    credit[0] -= 1.0
                    dve_bg.pop(0)()
                    if force:
                        break
                while pool_bg and (force or credit[1] >= 1.0):
                    credit[1] -= 1.0
                    pool_bg.pop(0)()
                    if force:
                        break

            def tap_op(eng, e, j, k):
                u = ua[j % NUA]
                uBf = uaB[j % NUA]
                src = u[:, H - (KA - 1) + k:H - (KA - 1) + k + T]
                op(eng, lambda: e.scalar_tensor_tensor(UC[:, j, :], src, cvec[:, j, k:k + 1], UC[:, j, :], ALU.mult, ALU.add),
                   reads=[uBf, constB, ucB[j]], writes=[ucB[j]])

            def pool_tap(j, k, th):
                u = ua[j % NUA]
                uBf = uaB[j % NUA]
                o = H - (KA - 1) + k + th * 1024
                op(POOL, lambda: nc.gpsimd.tensor_scalar(tmpP[:], u[:, o:o + 1024], cvec[:, j, k:k + 1], None, ALU.mult),
                   reads=[uBf, constB], writes=[tmpPB])
                op(POOL, lambda: nc.gpsimd.tensor_tensor(UC[:, j, th * 1024:(th + 1) * 1024], UC[:, j, th * 1024:(th + 1) * 1024], tmpP[:], ALU.add),
                   reads=[tmpPB, ucB[j]], writes=[ucB[j]])

            def sched_conv(j):
                nP, nQ = NP_TAPS[j], NQ_TAPS[j]
                dtaps = list(range(nP, KA - nQ))
                qtaps = list(range(KA - nQ, KA))

                def enqueue_pool():
                    for k in qtaps:
                        for th in range(2):
                            pool_bg.append(lambda k=k, th=th: pool_tap(j, k, th))
                for idx, k in enumerate(dtaps):
                    dve_bg.append(lambda k=k: tap_op(DVE, nc.vector, j, k))
                if qtaps:
                    dve_bg.append(enqueue_pool)
                dve_bg.append(lambda: stats_ready.append(j))

            def stats_act(j, th):
                tsl = slice(th * 1024, (th + 1) * 1024)
                op(ACT, lambda: nc.scalar.activation(ucq[:], UC[:, j, tsl], AF.Square), reads=[ucB[j]], writes=[ucqB])
                op(ACT, lambda: nc.scalar.copy(ucb[:], UC[:, j, tsl]), reads=[ucB[j]], writes=[ucbB])

            def stats_pe(j, th):
                ps, psB = next_slot()

                def emit_st():
                    last = None
                    for c in range(8):
                        nc.tensor.matmul(ps[:, 2 * c:2 * c + 1], ucb[:, c * 128:(c + 1) * 128], onesb[:, 0:1], start=True, stop=True)
                        last = nc.tensor.matmul(ps[:, 2 * c + 1:2 * c + 2], ucq[:, c * 128:(c + 1) * 128], onesb[:, 0:1], start=True, stop=True)
                    return last
                op(PE, emit_st, reads=[ucbB, ucqB, constB], writes=[psB])
                op(ACT, lambda: nc.scalar.copy(stt[:], ps[:, 0:16]), reads=[psB], writes=[sttB])
                op(DVE, lambda: nc.vector.tensor_tensor(st[:, th * 16:(th + 1) * 16], stt[:], st[:, th * 16:(th + 1) * 16], ALU.add),
                   reads=[sttB, statsB], writes=[statsB])

            def sched_stats(j):
                def mid():
                    stats_pe(j, 0)
                    stats_act(j, 1)
                after_pumps(4, lambda: stats_act(j, 0))
                after_pumps(8, mid)
                after_pumps(12, lambda: stats_pe(j, 1))

            def emit_A1(j):
                u = ua[j % NUA]
                uBf = uaB[j % NUA]
                nP = NP_TAPS[j]
                sG = acquire_tile(3)
                sV = acquire_tile(3)
                ps, psB = halo_unit(sG)
                op(ACT, lambda: nc.scalar.activation(sgh[:], ps[:, 0:H], AF.Sigmoid), reads=[psB], writes=[sghB])
                ps, psB = halo_unit(sV)
                op(ACT, lambda: nc.scalar.copy(hv[:], ps[:, 0:H]), reads=[psB], writes=[hvB])
                op(DVE, lambda: nc.vector.tensor_tensor(u[:, 0:H], hv[:], sgh[:], ALU.mult),
                   reads=[hvB, sghB], writes=[uBf])
                for th in range(2):
                    ps, psB = main_unit(sG, XhT, hTB, th * 1024)
                    op(ACT, lambda: nc.scalar.activation(sg[th][:], ps[:], AF.Sigmoid), reads=[psB], writes=[sgB[th]])
                    pump()
                    ps, psB = main_unit(sV, XhT, hTB, th * 1024)
                    ev, eB = next_ev()
                    op(ACT, lambda: nc.scalar.copy(ev[:], ps[:]), reads=[psB], writes=[eB])
                    defer(lambda ev=ev, eB=eB, th=th: op(
                        DVE, lambda: nc.vector.tensor_tensor(u[:, H + th * 1024:H + (th + 1) * 1024], ev[:], sg[th][:], ALU.mult),
                        reads=[eB, sgB[th]], writes=[uBf]))
                    pump()

            def build_dg(j):
                nP = NP_TAPS[j]
                in0 = bass.AP(idb, 0, [[128, 128], [0, nP], [1, 128]])
                in1 = bass.AP(cvec, j * NCV, [[NJ * NCV, 128], [1, nP], [0, 128]])
                op(DVE, lambda: nc.vector.tensor_tensor(dg[:, 0:nP, :], in0, in1, ALU.mult), reads=[constB], writes=[dgB])

            def emit_A1conv(j):
                u = ua[j % NUA]
                uBf = uaB[j % NUA]
                nP = NP_TAPS[j]
                bias = cvec[:, j, 31:32]
                for th in range(2):
                    ps, psB = conv_unit(dg, dgB, list(range(nP)), KA, u, uBf, th)
                    op(ACT, lambda: nc.scalar.activation(UC[:, j, th * 1024:(th + 1) * 1024], ps[:], AF.Identity, bias=bias),
                       reads=[psB, constB], writes=[ucB[j]])
                    pump()
                sched_conv(j)

            def emit_B(j, mid_hook=None):
                if stats_ready and not timed:
                    sched_stats(stats_ready.pop(0))
                in0 = bass.AP(idb, 0, [[128, 128], [0, KB], [1, 128]])
                in1 = bass.AP(cvec, j * NCV + 35, [[NJ * NCV, 128], [1, KB], [0, 128]])
                op(DVE, lambda: nc.vector.tensor_tensor(dg3[:], in0, in1, ALU.mult), reads=[constB], writes=[dg3B])
                sC = acquire_tile(3)
                sX = acquire_tile(3)
                sBt = acquire_tile(2)
                sZ = acquire_tile(2)
                ps, psB = halo_unit(sC)
                op(ACT, lambda: nc.scalar.copy(bch[:], ps[:, 0:H]), reads=[psB], writes=[bchB])
                ps, psB = halo_unit(sX)
                op(ACT, lambda: nc.scalar.copy(hv[:], ps[:, 0:H]), reads=[psB], writes=[hvB])
                op(DVE, lambda: nc.vector.tensor_tensor(cxb[:, 0:H], hv[:], bch[:], ALU.mult),
                   reads=[hvB, bchB], writes=[cxbB])
                for th in range(2):
                    ps, psB = main_unit(sC, XhT, hTB, th * 1024)
                    op(ACT, lambda: nc.scalar.copy(bc[th][:], ps[:]), reads=[psB], writes=[bcB[th]])
                    pump()
                    ps, psB = main_unit(sX, XhT, hTB, th * 1024)
                    ev, eB = next_ev()
                    op(ACT, lambda: nc.scalar.copy(ev[:], ps[:]), reads=[psB], writes=[eB])
                    op(DVE, lambda: nc.vector.tensor_tensor(cxb[:, H + th * 1024:H + (th + 1) * 1024], ev[:], bc[th][:], ALU.mult),
                       reads=[eB, bcB[th]], writes=[cxbB])
                    pump()
                if mid_hook is not None:
                    mid_hook()
                for th in range(2):
                    ps, psB = main_unit(sBt, XhT, hTB, th * 1024)
                    ev, eB = next_ev()
                    op(ACT, lambda: nc.scalar.copy(ev[:], ps[:]), reads=[psB], writes=[eB])
                    pump()
                    ps, psB = main_unit(sZ, XhT, hTB, th * 1024)
                    op(ACT, lambda: nc.scalar.activation(sbz[th][:], ps[:], AF.Silu), reads=[psB], writes=[sbzB[th]])
                    defer(lambda ev=ev, eB=eB, th=th: op(
                        DVE, lambda: nc.vector.tensor_tensor(sbz[th][:], ev[:], sbz[th][:], ALU.mult),
                        reads=[eB, sbzB[th]], writes=[sbzB[th]]))
                    pump()
                for th in range(2):
                    ps, psB = conv_unit(dg3, dg3B, list(range(KB)), KB, cxb, cxbB, th)
                    ev, eB = next_ev()
                    op(ACT, lambda: nc.scalar.copy(ev[:], ps[:]), reads=[psB], writes=[eB])
                    defer(lambda ev=ev, eB=eB, th=th: op(
                        DVE, lambda: nc.vector.tensor_tensor(VB[:, j, th * 1024:(th + 1) * 1024], ev[:], sbz[th][:], ALU.mult),
                        reads=[eB, sbzB[th]], writes=[vbB]))
                    pump()

            for step in range(NJ + LEAD):
                if stats_ready and not timed:
                    sched_stats(stats_ready.pop(0))
                if step < NJ:
                    emit_A1(step)
                    if step == LEAD:
                        for j0 in range(LEAD):
                            build_dg(j0)
                            emit_A1conv(j0)
                if step >= LEAD:
                    if step < NJ:
                        build_dg(step)
                    emit_B(step - LEAD, mid_hook=(lambda st_=step: emit_A1conv(st_)) if step < NJ else None)
            while dve_bg or pool_bg or lag_q[0] or lag_q[1] or timed:
                pump(force=True)
            while stats_ready:
                j_ = stats_ready.pop(0)
                for th_ in range(2):
                    stats_act(j_, th_)
                    stats_pe(j_, th_)
            barrier()

        with ExitStack() as pc:
            rbc = sb(pc, "rbc", [128, T], F32)
            nbc = sb(pc, "nbc", [128, T], F32)
            rbcB, nbcB = Buf(), Buf()
            zt = [sb(pc, f"zt{i}", [128, 1024], F32) for i in range(2)]
            sz = [sb(pc, f"sz{i}", [128, 1024], F32) for i in range(2)]
            ztB = [Buf() for _ in range(2)]
            szB = [Buf() for _ in range(2)]
            rhsR = sb(pc, "rhsR", [128, T], F32)
            rhsN = sb(pc, "rhsN", [128, T], F32)
            units = [(j, th) for j in range(NJ) for th in range(2)]
            tile_of = {}

            done1a = set()

            def a2_stage1a(u):
                if u in done1a:
                    return
                done1a.add(u)
                j, th = units[u]
                q = u % 2
                if th == 0:
                    tile_of[j] = acquire_tile(2)
                ps, psB = main_unit(tile_of[j], XhT, hTB, th * 1024)
                op(ACT, lambda: nc.scalar.activation(sz[q][:], ps[:], AF.Silu), reads=[psB], writes=[szB[q]])

            def a2_stage1(u):
                a2_stage1a(u)
                j, th = units[u]
                q = u % 2
                tsl = slice(th * 1024, (th + 1) * 1024)
                op(DVE, lambda: nc.vector.tensor_tensor(zt[q][:], UC[:, j, tsl], rbc[:, tsl], ALU.mult),
                   reads=[ucB[j], rbcB], writes=[ztB[q]])
                op(DVE, lambda: nc.vector.tensor_tensor(zt[q][:], zt[q][:], nbc[:, tsl], ALU.add),
                   reads=[nbcB, ztB[q]], writes=[ztB[q]])
                op(ACT, lambda: nc.scalar.activation(zt[q][:], zt[q][:], AF.Silu, bias=cvec[:, j, 33:34], scale=cvec[:, j, 32:33]),
                   reads=[ztB[q], constB], writes=[ztB[q]])

            def a2_stage2(u):
                j, th = units[u]
                q = u % 2
                op(DVE, lambda: nc.vector.tensor_tensor(UCb[:, j, th * 1024:(th + 1) * 1024], zt[q][:], sz[q][:], ALU.mult),
                   reads=[ztB[q], szB[q], ucB[j]], writes=[vaB, ucB[j]])

            stB = statsB
            st3 = st[:].rearrange("p (c q) -> p c q", q=2)
            mu, msq, var, rstd, nmr = (stw[:, i, :] for i in range(5))
            op(DVE, lambda: nc.vector.tensor_scalar(mu, st3[:, :, 0], 1.0 / D, None, ALU.mult), reads=[stB], writes=[stB])
            op(DVE, lambda: nc.vector.tensor_tensor(msq, mu, mu, ALU.mult), reads=[stB], writes=[stB])
            op(DVE, lambda: nc.vector.scalar_tensor_tensor(var, st3[:, :, 1], 1.0 / D, msq, ALU.mult, ALU.subtract), reads=[stB], writes=[stB])
            op(ACT, lambda: nc.scalar.activation(rstd, var, AF.Sqrt, bias=epsc[:], scale=1.0), reads=[stB, constB], writes=[stB])
            op(DVE, lambda: nc.vector.reciprocal(rstd, rstd), reads=[stB], writes=[stB])
            op(DVE, lambda: nc.vector.scalar_tensor_tensor(nmr, mu, -1.0, rstd, ALU.mult, ALU.mult), reads=[stB], writes=[stB])
            rB, nB = Buf(), Buf()
            idbc = bass.AP(idf, 0, [[128, 128], [0, 16], [1, 128]])
            op(DVE, lambda: nc.vector.tensor_tensor(rhsR[:].rearrange("p (c t) -> p c t", c=16), idbc,
                                                    bass.AP(stw, 3 * 16, [[96, 128], [1, 16], [0, 128]]), ALU.mult),
               reads=[stB, idB], writes=[rB])
            op(DVE, lambda: nc.vector.tensor_tensor(rhsN[:].rearrange("p (c t) -> p c t", c=16), idbc,
                                                    bass.AP(stw, 4 * 16, [[96, 128], [1, 16], [0, 128]]), ALU.mult),
               reads=[stB, idB], writes=[nB])
            a2_stage1a(0)
            a2_stage1a(1)
            for (srcT, srcB, dst, dstB) in ((rhsR, rB, rbc, rbcB), (rhsN, nB, nbc, nbcB)):
                for th in range(2):
                    ps, psB = next_slot()

                    def emit_bc():
                        last = None
                        for b in range(2):
                            last = nc.tensor.matmul(ps[:, b * 512:(b + 1) * 512], onesf[:], srcT[:, th * 1024 + b * 512:th * 1024 + (b + 1) * 512],
                                                    start=True, stop=True)
                        return last
                    op(PE, emit_bc, reads=[srcB, constB], writes=[psB])
                    op(ACT, lambda: nc.scalar.copy(dst[:, th * 1024:(th + 1) * 1024], ps[:]), reads=[psB], writes=[dstB])
            a2_stage1(0)
            for u in range(len(units)):
                if u + 1 < len(units):
                    a2_stage1(u + 1)
                a2_stage2(u)

            mB = Buf("m")
            wo = sb(pc, "wo", [128, NJ, D], BF16)
            woB = Buf()
            wosem = new_dsem("wosem")
            if True:
                sga = [zt[0][:], zt[1][:]]
                sgaB = ztB
                sgb = [sz[0][:], sz[1][:]]
                sgbB = szB
                t1 = [rhsR[:, 0:1024], rhsR[:, 1024:2048]]
                t1B = [Buf() for _ in range(2)]
                for b_ in t1B:
                    b_.w = rB.w
                    b_.r = dict(rB.r)
                mBs = [Buf("m0"), Buf("m1")]

                g2 = nbc[:, 1024:2048]
                g2B = Buf()
                g2B.w, g2B.r = nbcB.w, dict(nbcB.r)
                c3 = new_dsem("c3")
                xr = [rbc[:, 0:1024], rbc[:, 1024:2048], nbc[:, 0:1024]]
                xrB = [Buf() for _ in range(3)]
                for b_, src_ in zip(xrB, (rbcB, rbcB, nbcB)):
                    b_.w, b_.r = src_.w, dict(src_.r)
                xrsem = [new_dsem(f"xrsem{i}") for i in range(3)]
                yo = [rhsN[:, 0:1024], rhsN[:, 1024:2048]]
                yoB = [Buf() for _ in range(2)]
                for b_ in yoB:
                    b_.w, b_.r = nB.w, dict(nB.r)
                yosem = [new_dsem(f"yosem{i}") for i in range(2)]

                def load_xr(t_):
                    s_ = t_ % 3
                    dma(SP, xrsem[s_], lambda: nc.sync.dma_start(out=xr[s_], in_=xm[t_ * 128:(t_ + 1) * 128, :]), writes=[xrB[s_]])

                def u_ga(i, th, q, sGa):
                    ps, psB = main_unit(sGa, XhT, hTB, th * 1024)
                    op(ACT, lambda: nc.scalar.activation(sga[q], ps[:], AF.Sigmoid), reads=[psB], writes=[sgaB[q]])

                def u_ya(i, th, q, sWa):
                    ps, psB = main_unit(sWa, Xva, vaB, th * 1024)
                    op(DVE, lambda: nc.vector.scalar_tensor_tensor(t1[q], ps[:], cvec[:, i, 34:35], sga[q], ALU.add, ALU.mult),
                       reads=[psB, sgaB[q], constB], writes=[t1B[q]])

                def u_gb(i, th, q, sGb):
                    ps, psB = main_unit(sGb, XhT, hTB, th * 1024)
                    op(ACT, lambda: nc.scalar.activation(sgb[q], ps[:], AF.Sigmoid), reads=[psB], writes=[sgbB[q]])

                def u_yb(i, th, q, sWb):
                    ps, psB = main_unit(sWb, Xvb, vbB, th * 1024)
                    op(DVE, lambda: nc.vector.tensor_tensor(sgb[q], ps[:], sgb[q], ALU.mult),
                       reads=[psB, sgbB[q]], writes=[sgbB[q]])

                def u_m(i, th, q):
                    op(DVE, lambda: nc.vector.tensor_tensor(UCb[:, i, T + th * 1024:T + (th + 1) * 1024], t1[q], sgb[q], ALU.add),
                       reads=[t1B[q], sgbB[q]], writes=[mBs[th]])

                for i in range(NJ):
                    sGa = acquire_tile(2)
                    sWa = acquire_tile(2)
                    sGb = acquire_tile(2)
                    sWb = acquire_tile(2)
                    if i == NJ - 1:
                        dma(POOL, wosem, lambda: nc.gpsimd.dma_start(out=wo[:], in_=w_o_v), writes=[woB])
                        dma(SP, c3, lambda: nc.sync.dma_start(out=g2, in_=g2_d), writes=[g2B])
                        for t_ in range(3):
                            load_xr(t_)
                    if i == 0:
                        for th in range(2):
                            u_ga(i, th, th, sGa)
                        for th in range(2):
                            u_gb(i, th, th, sGb)
                        for th in range(2):
                            u_yb(i, th, th, sWb)
                        for th in range(2):
                            u_ya(i, th, th, sWa)
                            u_m(i, th, th)
                    else:
                        for th in range(2):
                            q = th
                            u_ga(i, th, q, sGa)
                            u_ya(i, th, q, sWa)
                            u_gb(i, th, q, sGb)
                            u_yb(i, th, q, sWb)
                            u_m(i, th, q)
            junk2 = rhsR[:].bitcast(BF16)[:, 0:D]
            junk2B = Buf()
            junk2B.w, junk2B.r = t1B[0].w, dict(t1B[0].r)
            ss2 = ss1[:, 0:16]
            rs2 = rs1[:, 0:16]
            ss2B = Buf()
            op(DVE, lambda: nc.vector.memset(ss2, 0.0), writes=[ss2B])
            rbs = [Buf() for _ in range(16)]

            def fin_stage1(tc):
                s3 = tc % 3
                ps, psB = next_slot()

                def emit_f():
                    last = None
                    for n in range(2):
                        for k in range(NJ):
                            last = nc.tensor.matmul(ps[:, n * 512:(n + 1) * 512], UCb[:, k, T + tc * 128:T + (tc + 1) * 128],
                                                    wo[:, k, n * 512:(n + 1) * 512], start=(k == 0), stop=(k == NJ - 1))
                    return last
                op(PE, emit_f, reads=[mBs[tc // 8], woB], writes=[psB])
                op(DVE, lambda: nc.vector.tensor_tensor(xr[s3], ps[:], xr[s3], ALU.add), reads=[psB, xrB[s3]], writes=[xrB[s3]])
                op(ACT, lambda: nc.scalar.activation(junk2, xr[s3], AF.Square, accum_out=ss2[:, tc:tc + 1]),
                   reads=[xrB[s3], ss2B], writes=[junk2B])
                op(ACT, lambda: nc.scalar.activation(rs2[:, tc:tc + 1], ss2[:, tc:tc + 1], AF.Sqrt, bias=epsc[:], scale=1.0 / D),
                   reads=[junk2B, constB], writes=[rbs[tc]])

            def fin_stage2(tc):
                s3 = tc % 3
                s2 = tc % 2
                rb = rbs[tc]
                op(DVE, lambda: nc.vector.reciprocal(rs2[:, tc:tc + 1], rs2[:, tc:tc + 1]),
                   reads=[rb], writes=[rb])
                op(DVE, lambda: nc.vector.scalar_tensor_tensor(yo[s2], xr[s3], rs2[:, tc:tc + 1], g2, ALU.mult, ALU.mult),
                   reads=[rb, xrB[s3], g2B], writes=[yoB[s2]])
                dma(SP, yosem[s2], lambda: nc.sync.dma_start(out=y[tc * 128:(tc + 1) * 128, :], in_=yo[s2]), reads=[yoB[s2]])
                if tc + 3 < 16:
                    load_xr(tc + 3)

            fin_stage1(0)
            for tc in range(16):
                if tc + 1 < 16:
                    fin_stage1(tc + 1)
                fin_stage2(tc)
            barrier()
    return nc


_NC_CACHE = {}


def kernel(x, meta_tokens, norm_g, w_in, conv_a_w, conv_a_b, ln_a_g, ln_a_b,
           w_a_out, b_a_out, conv_b_w, w_b_out, w_out, final_g):
    f = np.float32
    x = np.asarray(x, f)
    meta = np.asarray(meta_tokens, f)
    B = x.shape[0]
    ncores = 8
    pk = np.concatenate([np.asarray(conv_a_w, f)[0], np.asarray(conv_a_b, f), np.asarray(ln_a_g, f),
                         np.asarray(ln_a_b, f), np.asarray(b_a_out, f), np.asarray(conv_b_w, f)[0]], axis=0)
    assert pk.shape == (NCV, D)
    cvec = np.ascontiguousarray(pk.reshape(NCV, NJ, 128).transpose(2, 1, 0)).reshape(128, NJ * NCV)
    g1 = np.ascontiguousarray(np.broadcast_to(np.asarray(norm_g, f).reshape(1, D), (128, D)))
    g2 = np.ascontiguousarray(np.broadcast_to(np.asarray(final_g, f).reshape(1, D), (128, D)))
    ident = np.eye(128, dtype=f)
    wi = np.ascontiguousarray(np.asarray(w_in, f)[0])
    wa = np.ascontiguousarray(np.asarray(w_a_out, f)[0])
    wb = np.ascontiguousarray(np.asarray(w_b_out, f)[0])
    wo = np.ascontiguousarray(np.asarray(w_out, f)[0])
    in_maps = []
    for i in range(ncores):
        b, hf = i // 2, i % 2
        xmi = np.ascontiguousarray(x[b, hf * T:(hf + 1) * T])
        if hf == 0:
            xhi = np.concatenate([np.zeros((H - 16, D), f), meta], axis=0)
        else:
            xhi = np.ascontiguousarray(x[b, T - H:T])
        in_maps.append({"xm": xmi, "xh": xhi, "w_in": wi, "w_a": wa, "w_b": wb, "w_o": wo,
                        "cvec": cvec, "g1": g1, "g2": g2, "ident": ident})
    if "nc" not in _NC_CACHE:
        _NC_CACHE["nc"] = build_nc()
    nc = _NC_CACHE["nc"]
    res = run_bass_kernel_spmd(nc, in_maps, core_ids=list(range(ncores)))
    out = np.empty((B, 2 * T, D), f)
    for i in range(ncores):
        b, hf = i // 2, i % 2
        out[b, hf * T:(hf + 1) * T] = res.results[i]["y"]
    return out
```

```python
import numpy as np
from contextlib import ExitStack

import concourse.bass as bass
import concourse.mybir as mybir
from concourse.bass_utils import run_bass_kernel_spmd

F32 = mybir.dt.float32
BF16 = mybir.dt.bfloat16
AF = mybir.ActivationFunctionType
ALU = mybir.AluOpType

D = 1024
T = 2048
H = 32
TT = T + H
NJ = 8
KA = 31
KB = 3
EPS = 1e-6
NCV = 38
NWT = 6


class SemSrc:
    def __init__(self, nc, ctx, name, step):
        self.sem = ctx.enter_context(nc.semaphore(name))
        self.count = 0
        self.step = step
        self.name = name

    def mark(self, ins):
        ins.then_inc(self.sem, self.step)
        self.count += self.step
        return (self, self.count)


class Eng(SemSrc):
    def __init__(self, nc, ctx, e, name, kind):
        super().__init__(nc, ctx, "s_" + name, 1)
        self.e = e
        self.kind = kind
        self.waited = {}

    def wait(self, ev):
        src, val = ev
        if src is self:
            if self.kind == "pe":
                return
        if self.waited.get(src, 0) >= val:
            return
        self.e.wait_ge(src.sem, val)
        self.waited[src] = val


class Buf:
    def __init__(self, name=""):
        self.name = name
        self.w = None
        self.r = {}


def _deps(reads, writes):
    evs = []
    for b in reads:
        if b.w is not None:
            evs.append(b.w)
    for b in writes:
        if b.w is not None:
            evs.append(b.w)
        evs += list(b.r.items())
    return evs


def _commit(ev, reads, writes):
    for b in reads:
        b.r[ev[0]] = max(b.r.get(ev[0], 0), ev[1])
    for b in writes:
        b.w = ev
        b.r = {}


def op(eng, fn, reads=(), writes=()):
    for ev in _deps(reads, writes):
        eng.wait(ev)
    ins = fn()
    ev = eng.mark(ins)
    _commit(ev, reads, writes)
    return ev


def dma(qeng, dsem, fn, reads=(), writes=()):
    for ev in _deps(reads, writes):
        qeng.wait(ev)
    ins = fn()
    ev = dsem.mark(ins)
    _commit(ev, reads, writes)
    return ev


NP_TAPS = [13, 13, 13, 13, 13, 13, 18, 24]
NQ_TAPS = [0, 0, 0, 0, 0, 0, 0, 0]
NPMAX = max(NP_TAPS)
DVE_RATE = 1.2
POOL_RATE = 0.8
R_N, R_M, R_C, R_3 = 1.1, 1.1, 1.1, 1.1


def build_nc():
    nc = bass.Bass("TRN2", target_bir_lowering=False)
    xm = nc.dram_tensor("xm", [T, D], F32, kind="ExternalInput").ap()
    xh = nc.dram_tensor("xh", [H, D], F32, kind="ExternalInput").ap()
    w_in = nc.dram_tensor("w_in", [D, 9 * D], F32, kind="ExternalInput").ap()
    w_a = nc.dram_tensor("w_a", [D, D], F32, kind="ExternalInput").ap()
    w_b = nc.dram_tensor("w_b", [D, D], F32, kind="ExternalInput").ap()
    w_o = nc.dram_tensor("w_o", [D, D], F32, kind="ExternalInput").ap()
    cvec_d = nc.dram_tensor("cvec", [128, NJ * NCV], F32, kind="ExternalInput").ap()
    g1_d = nc.dram_tensor("g1", [128, D], F32, kind="ExternalInput").ap()
    g2_d = nc.dram_tensor("g2", [128, D], F32, kind="ExternalInput").ap()
    id_d = nc.dram_tensor("ident", [128, 128], F32, kind="ExternalInput").ap()
    y = nc.dram_tensor("y", [T, D], F32, kind="ExternalOutput").ap()

    w_in_v = w_in.rearrange("(k p) c -> p k c", p=128)
    w_a_v = w_a.rearrange("(k p) c -> p k c", p=128)
    w_b_v = w_b.rearrange("(k p) c -> p k c", p=128)
    w_o_v = w_o.rearrange("(k p) c -> p k c", p=128)

    with ExitStack() as ctx:
        PE = Eng(nc, ctx, nc.tensor, "pe", "pe")
        ACT = Eng(nc, ctx, nc.scalar, "act", "act")
        DVE = Eng(nc, ctx, nc.vector, "dve", "dve")
        POOL = Eng(nc, ctx, nc.gpsimd, "pool", "pool")
        SP = Eng(nc, ctx, nc.sync, "sp", "sp")
        engines = [PE, ACT, DVE, POOL, SP]
        dsems = []

        def new_dsem(name):
            s = SemSrc(nc, ctx, name, 16)
            dsems.append(s)
            return s

        def barrier():
            srcs = engines + dsems
            for e in engines:
                for s in srcs:
                    if s is e or s.count == 0:
                        continue
                    e.wait((s, s.count))

        def sb(c, name, shape, dt):
            return c.enter_context(nc.sbuf_tensor(name, shape, dt))

        hT = sb(ctx, "hT", [128, NJ, TT], BF16)
        UC = sb(ctx, "UC", [128, NJ, T], F32)
        UCb = UC[:].bitcast(BF16)
        VB = sb(ctx, "VB", [128, NJ, T], BF16)
        wt = [sb(ctx, f"wt{i}", [128, NJ, 128], BF16) for i in range(NWT)]
        cvec = sb(ctx, "cvec_s", [128, NJ, NCV], F32)
        idf = sb(ctx, "idf", [128, 128], F32)
        idb = sb(ctx, "idb", [128, 128], BF16)
        onesb = sb(ctx, "onesb", [128, 2], BF16)
        onesf = sb(ctx, "onesf", [128, 128], F32)
        ss1 = sb(ctx, "ss1", [128, 32], F32)
        rs1 = sb(ctx, "rs1", [128, 32], F32)
        st = sb(ctx, "st", [128, 32], F32)
        stw = sb(ctx, "stw", [128, 6, 16], F32)
        epsc = sb(ctx, "epsc", [128, 1], F32)

        NRING = 4
        ring = [ctx.enter_context(nc.psum_tensor(f"ring{i}", [128, 1024], F32)) for i in range(NRING)]
        ringB = [Buf(f"ring{i}") for i in range(NRING)]
        statsB = Buf("stats")
        ucnt = [0]

        def next_slot():
            s = ucnt[0] % NRING
            ucnt[0] += 1
            return ring[s], ringB[s]

        wtB = [Buf(f"wt{i}") for i in range(NWT)]
        wsem = [new_dsem(f"wsem{i}") for i in range(NWT)]
        csem = new_dsem("csem")
        hTB = Buf("hT")
        constB = Buf("const")

        tiles = []

        def G(g, j):
            return (w_in_v, g * D + j * 128)

        LEAD = 1
        for step in range(NJ + LEAD):
            if step < NJ:
                tiles += [G(1, step), G(0, step)]
            if step >= LEAD:
                j = step - LEAD
                tiles += [G(4, j), G(5, j), G(3, j), G(6, j)]
        for j in range(NJ):
            tiles += [G(2, j)]
        for i in range(NJ):
            tiles += [G(7, i), (w_a_v, i * 128), G(8, i), (w_b_v, i * 128)]
        tstate = {"issued": 0, "next_use": 0, "slot_users_left": [0] * NWT}

        def issue_tiles():
            while tstate["issued"] < len(tiles):
                i = tstate["issued"]
                s = i % NWT
                if tstate["slot_users_left"][s] != 0:
                    break
                view, c0 = tiles[i]
                dma(POOL, wsem[s],
                    lambda: nc.gpsimd.dma_start(out=wt[s][:], in_=view[:, :, c0:c0 + 128]),
                    writes=[wtB[s]])
                tstate["slot_users_left"][s] = -1
                tstate["issued"] += 1

        def acquire_tile(nusers):
            i = tstate["next_use"]
            tstate["next_use"] += 1
            assert i < tstate["issued"], "tile not issued"
            s = i % NWT
            tstate["slot_users_left"][s] = nusers
            return s

        def release_tile(s):
            tstate["slot_users_left"][s] -= 1
            if tstate["slot_users_left"][s] == 0:
                issue_tiles()

        c4 = new_dsem("c4")
        idB = Buf()
        dma(SP, csem, lambda: nc.sync.dma_start(out=cvec[:], in_=cvec_d.rearrange("p (j r) -> p j r", j=NJ)), writes=[constB])
        dma(SP, c4, lambda: nc.sync.dma_start(out=idf[:], in_=id_d), writes=[idB])
        issue_tiles()
        op(DVE, lambda: nc.vector.tensor_copy(idb[:], idf[:]), reads=[idB], writes=[constB])
        op(DVE, lambda: nc.vector.memset(onesb[:], 1.0), writes=[constB])
        op(DVE, lambda: nc.vector.memset(onesf[:], 1.0), writes=[constB])
        op(DVE, lambda: nc.vector.memset(ss1[:], 0.0), writes=[constB])
        op(DVE, lambda: nc.vector.memset(epsc[:], EPS), writes=[constB])
        op(DVE, lambda: nc.vector.memset(st[:], 0.0), writes=[statsB])

        with ExitStack() as pc:
            g1 = sb(pc, "g1_s", [128, D], F32)
            g1B = Buf()
            c2 = new_dsem("c2")
            dma(SP, c2, lambda: nc.sync.dma_start(out=g1[:], in_=g1_d), writes=[g1B])
            NXT = 6
            NHN = 4
            xt = [sb(pc, f"xt{i}", [128, D], F32) for i in range(NXT)]
            xtB = [Buf() for _ in range(NXT)]
            xsem = [new_dsem(f"xsem{i}") for i in range(NXT)]
            junk = sb(pc, "junk", [128, D], BF16)
            junkB = Buf()
            hn = [sb(pc, f"hn{i}", [128, D], BF16) for i in range(NHN)]
            hnB = [Buf() for _ in range(NHN)]

            def rows_of(c):
                return H if c == 0 else 128

            def stage_a(c):
                s3 = c % NXT
                sh = c % NHN
                rows = rows_of(c)
                src = xh if c == 0 else xm[(c - 1) * 128:c * 128, :]
                dma(SP, xsem[s3], lambda: nc.sync.dma_start(out=xt[s3][0:rows, :], in_=src), writes=[xtB[s3]])
                op(ACT, lambda: nc.scalar.activation(junk[0:rows, :], xt[s3][0:rows, :], AF.Square,
                                                     accum_out=ss1[0:rows, c:c + 1]),
                   reads=[xtB[s3], constB], writes=[junkB])
                rb = Buf()
                op(ACT, lambda: nc.scalar.activation(rs1[0:rows, c:c + 1], ss1[0:rows, c:c + 1], AF.Sqrt, bias=epsc[0:rows, :], scale=1.0 / D),
                   reads=[junkB, constB], writes=[rb])
                op(DVE, lambda: nc.vector.reciprocal(rs1[0:rows, c:c + 1], rs1[0:rows, c:c + 1]),
                   reads=[rb], writes=[rb])
                op(DVE, lambda: nc.vector.scalar_tensor_tensor(hn[sh][0:rows, :], xt[s3][0:rows, :], rs1[0:rows, c:c + 1],
                                                               g1[0:rows, :], ALU.mult, ALU.mult),
                   reads=[rb, xtB[s3], g1B], writes=[hnB[sh]])

            def stage_b(c):
                s3 = c % NHN
                rows = rows_of(c)
                tok0 = 0 if c == 0 else H + (c - 1) * 128
                ps, psB = next_slot()
                psb = ps[:].bitcast(BF16)

                def emit_tr():
                    last = None
                    for k in range(NJ):
                        last = nc.tensor.transpose(psb[:, k * 128:k * 128 + rows], hn[s3][0:rows, k * 128:(k + 1) * 128],
                                                   idb[0:rows, 0:rows])
                    return last
                op(PE, emit_tr, reads=[hnB[s3], constB], writes=[psB])
                src_ps = psb[:, 0:1024].rearrange("p (k t) -> p k t", k=NJ)[:, :, 0:rows]
                op(DVE, lambda: nc.vector.tensor_copy(hT[:, :, tok0:tok0 + rows], src_ps), reads=[psB], writes=[hTB])

            AHEAD = 3
            for c in range(AHEAD):
                stage_a(c)
            for c in range(17):
                if c + AHEAD < 17:
                    stage_a(c + AHEAD)
                stage_b(c)
            barrier()

        def main_unit(s, X, XB, tok0, extra_reads=()):
            ps, psB = next_slot()

            def emit():
                last = None
                for b in range(2):
                    for k in range(NJ):
                        last = nc.tensor.matmul(ps[:, b * 512:(b + 1) * 512], wt[s][:, k, :],
                                                X(k, tok0 + b * 512, 512), start=(k == 0), stop=(k == NJ - 1))
                return last
            op(PE, emit, reads=[wtB[s], XB] + list(extra_reads), writes=[psB])
            release_tile(s)
            return ps, psB

        def halo_unit(s):
            ps, psB = next_slot()

            def emit():
                last = None
                for k in range(NJ):
                    last = nc.tensor.matmul(ps[:, 0:H], wt[s][:, k, :], hT[:, k, 0:H], start=(k == 0), stop=(k == NJ - 1))
                return last
            op(PE, emit, reads=[wtB[s], hTB], writes=[psB])
            release_tile(s)
            return ps, psB

        def XhT(k, t0, n):
            return hT[:, k, H + t0:H + t0 + n]

        def Xva(k, t0, n):
            return UCb[:, k, t0:t0 + n]

        def Xvb(k, t0, n):
            return VB[:, k, t0:t0 + n]

        def conv_unit(dgt, dgB, taps, K, src, srcB, th):
            ps, psB = next_slot()

            def emit():
                last = None
                for b in range(2):
                    t0 = H + th * 1024 + b * 512 - (K - 1)
                    for i, k in enumerate(taps):
                        last = nc.tensor.matmul(ps[:, b * 512:(b + 1) * 512], dgt[:, i, :], src[:, t0 + k:t0 + k + 512],
                                                start=(i == 0), stop=(i == len(taps) - 1))
                return last
            op(PE, emit, reads=[dgB, srcB], writes=[psB])
            return ps, psB

        ucB = [Buf(f"uc{j}") for j in range(NJ)]
        vaB = Buf("va")
        vbB = Buf("vb")

        with ExitStack() as pc:
            NUA = 3
            ua = [sb(pc, f"ua{i}", [128, TT], BF16) for i in range(NUA)]
            uaB = [Buf(f"ua{i}") for i in range(NUA)]
            sg = [sb(pc, f"sg{i}", [128, 1024], F32) for i in range(2)]
            sgB = [Buf() for _ in range(2)]
            sgh = sb(pc, "sgh", [128, H], F32)
            sghB = Buf()
            hv = sb(pc, "hv", [128, H], F32)
            hvB = Buf()
            stt = sb(pc, "stt", [128, 16], F32)
            sttB = Buf()
            dg = sb(pc, "dg", [128, NPMAX, 128], BF16)
            dgB = Buf()
            ucb = sb(pc, "ucb", [128, 1024], BF16)
            ucq = sb(pc, "ucq", [128, 1024], BF16)
            ucbB, ucqB = Buf(), Buf()
            NEV = 3
            evt = [sb(pc, f"ev{i}", [128, 1024], F32) for i in range(NEV)]
            evB = [Buf() for _ in range(NEV)]
            evc = [0]

            def next_ev():
                e = evc[0] % NEV
                evc[0] += 1
                return evt[e], evB[e]
            bc = [sb(pc, f"bc{i}", [128, 1024], F32) for i in range(2)]
            bch = sb(pc, "bch", [128, H], F32)
            sbz = [sb(pc, f"sbz{i}", [128, 1024], F32) for i in range(2)]
            cxb = sb(pc, "cxb", [128, TT], BF16)
            dg3 = sb(pc, "dg3", [128, KB, 128], BF16)
            bcB = [Buf() for _ in range(2)]
            bchB = Buf()
            sbzB = [Buf() for _ in range(2)]
            cxbB = Buf()
            dg3B = Buf()

            dve_bg = []
            pool_bg = []

            credit = [0.0, 0.0]
            lag_q = [[], []]

            def defer(fn):
                lag_q[1].append(fn)

            timed = []
            stats_ready = []

            def after_pumps(n, fn):
                timed.append([n, fn])

            def pump(force=False, rate=None):
                for t_ in timed:
                    t_[0] -= 1
                due = [t_ for t_ in timed if t_[0] <= 0]
                for t_ in due:
                    timed.remove(t_)
                    t_[1]()
                for fn in lag_q[0]:
                    fn()
                lag_q[0] = lag_q[1]
                lag_q[1] = []
                credit[0] += DVE_RATE if rate is None else rate
                credit[1] += POOL_RATE
                while dve_bg and (force or credit[0] >= 1.0):
                    credit[0] -= 1.0
                    dve_bg.pop(0)()
                    if force:
                        break
                while pool_bg and (force or credit[1] >= 1.0):
                    credit[1] -= 1.0
                    pool_bg.pop(0)()
                    if force:
                        break

            def tap_op(eng, e, j, k):
                u = ua[j % NUA]
                uBf = uaB[j % NUA]
                src = u[:, H - (KA - 1) + k:H - (KA - 1) + k + T]
                op(eng, lambda: e.scalar_tensor_tensor(UC[:, j, :], src, cvec[:, j, k:k + 1], UC[:, j, :], ALU.mult, ALU.add),
                   reads=[uBf, constB, ucB[j]], writes=[ucB[j]])

            def pool_tap(j, k, th):
                u = ua[j % NUA]
                uBf = uaB[j % NUA]
                o = H - (KA - 1) + k + th * 1024
                op(POOL, lambda: nc.gpsimd.tensor_scalar(tmpP[:], u[:, o:o + 1024], cvec[:, j, k:k + 1], None, ALU.mult),
                   reads=[uBf, constB], writes=[tmpPB])
                op(POOL, lambda: nc.gpsimd.tensor_tensor(UC[:, j, th * 1024:(th + 1) * 1024], UC[:, j, th * 1024:(th + 1) * 1024], tmpP[:], ALU.add),
                   reads=[tmpPB, ucB[j]], writes=[ucB[j]])

            def sched_conv(j):
                nP, nQ = NP_TAPS[j], NQ_TAPS[j]
                dtaps = list(range(nP, KA - nQ))
                qtaps = list(range(KA - nQ, KA))

                def enqueue_pool():
                    for k in qtaps:
                        for th in range(2):
                            pool_bg.append(lambda k=k, th=th: pool_tap(j, k, th))
                for idx, k in enumerate(dtaps):
                    dve_bg.append(lambda k=k: tap_op(DVE, nc.vector, j, k))
                if qtaps:
                    dve_bg.append(enqueue_pool)
                dve_bg.append(lambda: stats_ready.append(j))

            def stats_act(j, th):
                tsl = slice(th * 1024, (th + 1) * 1024)
                op(ACT, lambda: nc.scalar.activation(ucq[:], UC[:, j, tsl], AF.Square), reads=[ucB[j]], writes=[ucqB])
                op(ACT, lambda: nc.scalar.copy(ucb[:], UC[:, j, tsl]), reads=[ucB[j]], writes=[ucbB])

            def stats_pe(j, th):
                ps, psB = next_slot()

                def emit_st():
                    last = None
                    for c in range(8):
                        nc.tensor.matmul(ps[:, 2 * c:2 * c + 1], ucb[:, c * 128:(c + 1) * 128], onesb[:, 0:1], start=True, stop=True)
                        last = nc.tensor.matmul(ps[:, 2 * c + 1:2 * c + 2], ucq[:, c * 128:(c + 1) * 128], onesb[:, 0:1], start=True, stop=True)
                    return last
                op(PE, emit_st, reads=[ucbB, ucqB, constB], writes=[psB])
                op(ACT, lambda: nc.scalar.copy(stt[:], ps[:, 0:16]), reads=[psB], writes=[sttB])
                op(DVE, lambda: nc.vector.tensor_tensor(st[:, th * 16:(th + 1) * 16], stt[:], st[:, th * 16:(th + 1) * 16], ALU.add),
                   reads=[sttB, statsB], writes=[statsB])

            def sched_stats(j):
                def mid():
                    stats_pe(j, 0)
                    stats_act(j, 1)
                after_pumps(4, lambda: stats_act(j, 0))
                after_pumps(8, mid)
                after_pumps(12, lambda: stats_pe(j, 1))

            def emit_A1(j):
                u = ua[j % NUA]
                uBf = uaB[j % NUA]
                nP = NP_TAPS[j]
                sG = acquire_tile(3)
                sV = acquire_tile(3)
                ps, psB = halo_unit(sG)
                op(ACT, lambda: nc.scalar.activation(sgh[:], ps[:, 0:H], AF.Sigmoid), reads=[psB], writes=[sghB])
                ps, psB = halo_unit(sV)
                op(ACT, lambda: nc.scalar.copy(hv[:], ps[:, 0:H]), reads=[psB], writes=[hvB])
                op(DVE, lambda: nc.vector.tensor_tensor(u[:, 0:H], hv[:], sgh[:], ALU.mult),
                   reads=[hvB, sghB], writes=[uBf])
                for th in range(2):
                    ps, psB = main_unit(sG, XhT, hTB, th * 1024)
                    op(ACT, lambda: nc.scalar.activation(sg[th][:], ps[:], AF.Sigmoid), reads=[psB], writes=[sgB[th]])
                    pump(rate=R_N)
                    ps, psB = main_unit(sV, XhT, hTB, th * 1024)
                    ev, eB = next_ev()
                    op(ACT, lambda: nc.scalar.copy(ev[:], ps[:]), reads=[psB], writes=[eB])
                    defer(lambda ev=ev, eB=eB, th=th: op(
                        DVE, lambda: nc.vector.tensor_tensor(u[:, H + th * 1024:H + (th + 1) * 1024], ev[:], sg[th][:], ALU.mult),
                        reads=[eB, sgB[th]], writes=[uBf]))
                    pump(rate=R_M)

            def build_dg(j):
                nP = NP_TAPS[j]
                in0 = bass.AP(idb, 0, [[128, 128], [0, nP], [1, 128]])
                in1 = bass.AP(cvec, j * NCV, [[NJ * NCV, 128], [1, nP], [0, 128]])
                op(DVE, lambda: nc.vector.tensor_tensor(dg[:, 0:nP, :], in0, in1, ALU.mult), reads=[constB], writes=[dgB])

            def emit_A1conv(j):
                u = ua[j % NUA]
                uBf = uaB[j % NUA]
                nP = NP_TAPS[j]
                bias = cvec[:, j, 31:32]
                for th in range(2):
                    ps, psB = conv_unit(dg, dgB, list(range(nP)), KA, u, uBf, th)
                    op(ACT, lambda: nc.scalar.activation(UC[:, j, th * 1024:(th + 1) * 1024], ps[:], AF.Identity, bias=bias),
                       reads=[psB, constB], writes=[ucB[j]])
                    pump(rate=R_C)
                sched_conv(j)

            def emit_B(j, mid_hook=None):
                if stats_ready and not timed:
                    sched_stats(stats_ready.pop(0))
                in0 = bass.AP(idb, 0, [[128, 128], [0, KB], [1, 128]])
                in1 = bass.AP(cvec, j * NCV + 35, [[NJ * NCV, 128], [1, KB], [0, 128]])
                op(DVE, lambda: nc.vector.tensor_tensor(dg3[:], in0, in1, ALU.mult), reads=[constB], writes=[dg3B])
                sC = acquire_tile(3)
                sX = acquire_tile(3)
                sBt = acquire_tile(2)
                sZ = acquire_tile(2)
                ps, psB = halo_unit(sC)
                op(ACT, lambda: nc.scalar.copy(bch[:], ps[:, 0:H]), reads=[psB], writes=[bchB])
                ps, psB = halo_unit(sX)
                op(ACT, lambda: nc.scalar.copy(hv[:], ps[:, 0:H]), reads=[psB], writes=[hvB])
                op(DVE, lambda: nc.vector.tensor_tensor(cxb[:, 0:H], hv[:], bch[:], ALU.mult),
                   reads=[hvB, bchB], writes=[cxbB])
                for th in range(2):
                    ps, psB = main_unit(sC, XhT, hTB, th * 1024)
                    op(ACT, lambda: nc.scalar.copy(bc[th][:], ps[:]), reads=[psB], writes=[bcB[th]])
                    pump(rate=R_N)
                    ps, psB = main_unit(sX, XhT, hTB, th * 1024)
                    ev, eB = next_ev()
                    op(ACT, lambda: nc.scalar.copy(ev[:], ps[:]), reads=[psB], writes=[eB])
                    op(DVE, lambda: nc.vector.tensor_tensor(cxb[:, H + th * 1024:H + (th + 1) * 1024], ev[:], bc[th][:], ALU.mult),
                       reads=[eB, bcB[th]], writes=[cxbB])
                    pump(rate=R_M)
                if mid_hook is not None:
                    mid_hook()
                for th in range(2):
                    ps, psB = main_unit(sBt, XhT, hTB, th * 1024)
                    ev, eB = next_ev()
                    op(ACT, lambda: nc.scalar.copy(ev[:], ps[:]), reads=[psB], writes=[eB])
                    pump(rate=R_N)
                    ps, psB = main_unit(sZ, XhT, hTB, th * 1024)
                    op(ACT, lambda: nc.scalar.activation(sbz[th][:], ps[:], AF.Silu), reads=[psB], writes=[sbzB[th]])
                    defer(lambda ev=ev, eB=eB, th=th: op(
                        DVE, lambda: nc.vector.tensor_tensor(sbz[th][:], ev[:], sbz[th][:], ALU.mult),
                        reads=[eB, sbzB[th]], writes=[sbzB[th]]))
                    pump(rate=R_M)
                for th in range(2):
                    ps, psB = conv_unit(dg3, dg3B, list(range(KB)), KB, cxb, cxbB, th)
                    ev, eB = next_ev()
                    op(ACT, lambda: nc.scalar.copy(ev[:], ps[:]), reads=[psB], writes=[eB])
                    defer(lambda ev=ev, eB=eB, th=th: op(
                        DVE, lambda: nc.vector.tensor_tensor(VB[:, j, th * 1024:(th + 1) * 1024], ev[:], sbz[th][:], ALU.mult),
                        reads=[eB, sbzB[th]], writes=[vbB]))
                    pump(rate=R_3)

            for step in range(NJ + LEAD):
                if stats_ready and not timed:
                    sched_stats(stats_ready.pop(0))
                if step < NJ:
                    emit_A1(step)
                    if step == LEAD:
                        for j0 in range(LEAD):
                            build_dg(j0)
                            emit_A1conv(j0)
                if step >= LEAD:
                    if step < NJ:
                        build_dg(step)
                    emit_B(step - LEAD, mid_hook=(lambda st_=step: emit_A1conv(st_)) if step < NJ else None)
            while dve_bg or pool_bg or lag_q[0] or lag_q[1] or timed:
                pump(force=True)
            while stats_ready:
                j_ = stats_ready.pop(0)
                for th_ in range(2):
                    stats_act(j_, th_)
                    stats_pe(j_, th_)
            barrier()

        with ExitStack() as pc:
            rbc = sb(pc, "rbc", [128, T], F32)
            nbc = sb(pc, "nbc", [128, T], F32)
            rbcB, nbcB = Buf(), Buf()
            zt = [sb(pc, f"zt{i}", [128, 1024], F32) for i in range(2)]
            sz = [sb(pc, f"sz{i}", [128, 1024], F32) for i in range(2)]
            ztB = [Buf() for _ in range(2)]
            szB = [Buf() for _ in range(2)]
            rhsR = sb(pc, "rhsR", [128, T], F32)
            rhsN = sb(pc, "rhsN", [128, T], F32)
            units = [(j, th) for j in range(NJ) for th in range(2)]
            tile_of = {}

            done1a = set()

            def a2_stage1a(u):
                if u in done1a:
                    return
                done1a.add(u)
                j, th = units[u]
                q = u % 2
                if th == 0:
                    tile_of[j] = acquire_tile(2)
                ps, psB = main_unit(tile_of[j], XhT, hTB, th * 1024)
                op(ACT, lambda: nc.scalar.activation(sz[q][:], ps[:], AF.Silu), reads=[psB], writes=[szB[q]])

            def a2_stage1(u):
                a2_stage1a(u)
                j, th = units[u]
                q = u % 2
                tsl = slice(th * 1024, (th + 1) * 1024)
                op(DVE, lambda: nc.vector.tensor_tensor(zt[q][:], UC[:, j, tsl], rbc[:, tsl], ALU.mult),
                   reads=[ucB[j], rbcB], writes=[ztB[q]])
                op(DVE, lambda: nc.vector.tensor_tensor(zt[q][:], zt[q][:], nbc[:, tsl], ALU.add),
                   reads=[nbcB, ztB[q]], writes=[ztB[q]])
                op(ACT, lambda: nc.scalar.activation(zt[q][:], zt[q][:], AF.Silu, bias=cvec[:, j, 33:34], scale=cvec[:, j, 32:33]),
                   reads=[ztB[q], constB], writes=[ztB[q]])

            def a2_stage2(u):
                j, th = units[u]
                q = u % 2
                op(DVE, lambda: nc.vector.tensor_tensor(UCb[:, j, th * 1024:(th + 1) * 1024], zt[q][:], sz[q][:], ALU.mult),
                   reads=[ztB[q], szB[q], ucB[j]], writes=[vaB, ucB[j]])

            stB = statsB
            st3 = st[:].rearrange("p (c q) -> p c q", q=2)
            mu, msq, var, rstd, nmr = (stw[:, i, :] for i in range(5))
            op(DVE, lambda: nc.vector.tensor_scalar(mu, st3[:, :, 0], 1.0 / D, None, ALU.mult), reads=[stB], writes=[stB])
            op(DVE, lambda: nc.vector.tensor_tensor(msq, mu, mu, ALU.mult), reads=[stB], writes=[stB])
            op(DVE, lambda: nc.vector.scalar_tensor_tensor(var, st3[:, :, 1], 1.0 / D, msq, ALU.mult, ALU.subtract), reads=[stB], writes=[stB])
            op(ACT, lambda: nc.scalar.activation(rstd, var, AF.Sqrt, bias=epsc[:], scale=1.0), reads=[stB, constB], writes=[stB])
            op(DVE, lambda: nc.vector.reciprocal(rstd, rstd), reads=[stB], writes=[stB])
            op(DVE, lambda: nc.vector.scalar_tensor_tensor(nmr, mu, -1.0, rstd, ALU.mult, ALU.mult), reads=[stB], writes=[stB])
            rB, nB = Buf(), Buf()
            idbc = bass.AP(idf, 0, [[128, 128], [0, 16], [1, 128]])
            op(DVE, lambda: nc.vector.tensor_tensor(rhsR[:].rearrange("p (c t) -> p c t", c=16), idbc,
                                                    bass.AP(stw, 3 * 16, [[96, 128], [1, 16], [0, 128]]), ALU.mult),
               reads=[stB, idB], writes=[rB])
            op(DVE, lambda: nc.vector.tensor_tensor(rhsN[:].rearrange("p (c t) -> p c t", c=16), idbc,
                                                    bass.AP(stw, 4 * 16, [[96, 128], [1, 16], [0, 128]]), ALU.mult),
               reads=[stB, idB], writes=[nB])
            a2_stage1a(0)
            a2_stage1a(1)
            for (srcT, srcB, dst, dstB) in ((rhsR, rB, rbc, rbcB), (rhsN, nB, nbc, nbcB)):
                for th in range(2):
                    ps, psB = next_slot()

                    def emit_bc():
                        last = None
                        for b in range(2):
                            last = nc.tensor.matmul(ps[:, b * 512:(b + 1) * 512], onesf[:], srcT[:, th * 1024 + b * 512:th * 1024 + (b + 1) * 512],
                                                    start=True, stop=True)
                        return last
                    op(PE, emit_bc, reads=[srcB, constB], writes=[psB])
                    op(ACT, lambda: nc.scalar.copy(dst[:, th * 1024:(th + 1) * 1024], ps[:]), reads=[psB], writes=[dstB])
            a2_stage1(0)
            for u in range(len(units)):
                if u + 1 < len(units):
                    a2_stage1(u + 1)
                a2_stage2(u)

            mB = Buf("m")
            wo = sb(pc, "wo", [128, NJ, D], BF16)
            woB = Buf()
            wosem = new_dsem("wosem")
            if True:
                sga = [zt[0][:], zt[1][:]]
                sgaB = ztB
                sgb = [sz[0][:], sz[1][:]]
                sgbB = szB
                t1 = [rhsR[:, 0:1024], rhsR[:, 1024:2048]]
                t1B = [Buf() for _ in range(2)]
                for b_ in t1B:
                    b_.w = rB.w
                    b_.r = dict(rB.r)
                mBs = [Buf("m0"), Buf("m1")]

                g2 = nbc[:, 1024:2048]
                g2B = Buf()
                g2B.w, g2B.r = nbcB.w, dict(nbcB.r)
                c3 = new_dsem("c3")
                xr = [rbc[:, 0:1024], rbc[:, 1024:2048], nbc[:, 0:1024]]
                xrB = [Buf() for _ in range(3)]
                for b_, src_ in zip(xrB, (rbcB, rbcB, nbcB)):
                    b_.w, b_.r = src_.w, dict(src_.r)
                xrsem = [new_dsem(f"xrsem{i}") for i in range(3)]
                yo = [rhsN[:, 0:1024], rhsN[:, 1024:2048]]
                yoB = [Buf() for _ in range(2)]
                for b_ in yoB:
                    b_.w, b_.r = nB.w, dict(nB.r)
                yosem = [new_dsem(f"yosem{i}") for i in range(2)]

                def load_xr(t_):
                    s_ = t_ % 3
                    dma(SP, xrsem[s_], lambda: nc.sync.dma_start(out=xr[s_], in_=xm[t_ * 128:(t_ + 1) * 128, :]), writes=[xrB[s_]])

                def u_ga(i, th, q, sGa):
                    ps, psB = main_unit(sGa, XhT, hTB, th * 1024)
                    op(ACT, lambda: nc.scalar.activation(sga[q], ps[:], AF.Sigmoid), reads=[psB], writes=[sgaB[q]])

                def u_ya(i, th, q, sWa):
                    ps, psB = main_unit(sWa, Xva, vaB, th * 1024)
                    op(DVE, lambda: nc.vector.scalar_tensor_tensor(t1[q], ps[:], cvec[:, i, 34:35], sga[q], ALU.add, ALU.mult),
                       reads=[psB, sgaB[q], constB], writes=[t1B[q]])

                def u_gb(i, th, q, sGb):
                    ps, psB = main_unit(sGb, XhT, hTB, th * 1024)
                    op(ACT, lambda: nc.scalar.activation(sgb[q], ps[:], AF.Sigmoid), reads=[psB], writes=[sgbB[q]])

                def u_yb(i, th, q, sWb):
                    ps, psB = main_unit(sWb, Xvb, vbB, th * 1024)
                    op(DVE, lambda: nc.vector.tensor_tensor(sgb[q], ps[:], sgb[q], ALU.mult),
                       reads=[psB, sgbB[q]], writes=[sgbB[q]])

                def u_m(i, th, q):
                    op(DVE, lambda: nc.vector.tensor_tensor(UCb[:, i, T + th * 1024:T + (th + 1) * 1024], t1[q], sgb[q], ALU.add),
                       reads=[t1B[q], sgbB[q]], writes=[mBs[th]])

                for i in range(NJ):
                    sGa = acquire_tile(2)
                    sWa = acquire_tile(2)
                    sGb = acquire_tile(2)
                    sWb = acquire_tile(2)
                    if i == NJ - 1:
                        dma(POOL, wosem, lambda: nc.gpsimd.dma_start(out=wo[:], in_=w_o_v), writes=[woB])
                        dma(SP, c3, lambda: nc.sync.dma_start(out=g2, in_=g2_d), writes=[g2B])
                        for t_ in range(3):
                            load_xr(t_)
                    if i == 0:
                        for th in range(2):
                            u_ga(i, th, th, sGa)
                        for th in range(2):
                            u_gb(i, th, th, sGb)
                        for th in range(2):
                            u_yb(i, th, th, sWb)
                        for th in range(2):
                            u_ya(i, th, th, sWa)
                            u_m(i, th, th)
                    else:
                        for th in range(2):
                            q = th
                            u_ga(i, th, q, sGa)
                            u_ya(i, th, q, sWa)
                            u_gb(i, th, q, sGb)
                            u_yb(i, th, q, sWb)
                            u_m(i, th, q)
            junk2 = rhsR[:].bitcast(BF16)[:, 0:D]
            junk2B = Buf()
            junk2B.w, junk2B.r = t1B[0].w, dict(t1B[0].r)
            ss2 = ss1[:, 0:16]
            rs2 = rs1[:, 0:16]
            ss2B = Buf()
            op(DVE, lambda: nc.vector.memset(ss2, 0.0), writes=[ss2B])
            rbs = [Buf() for _ in range(16)]

            def fin_stage1(tc):
                s3 = tc % 3
                ps, psB = next_slot()

                def emit_f():
                    last = None
                    for n in range(2):
                        for k in range(NJ):
                            last = nc.tensor.matmul(ps[:, n * 512:(n + 1) * 512], UCb[:, k, T + tc * 128:T + (tc + 1) * 128],
                                                    wo[:, k, n * 512:(n + 1) * 512], start=(k == 0), stop=(k == NJ - 1))
                    return last
                op(PE, emit_f, reads=[mBs[tc // 8], woB], writes=[psB])
                op(DVE, lambda: nc.vector.tensor_tensor(xr[s3], ps[:], xr[s3], ALU.add), reads=[psB, xrB[s3]], writes=[xrB[s3]])
                op(ACT, lambda: nc.scalar.activation(junk2, xr[s3], AF.Square, accum_out=ss2[:, tc:tc + 1]),
                   reads=[xrB[s3], ss2B], writes=[junk2B])
                op(ACT, lambda: nc.scalar.activation(rs2[:, tc:tc + 1], ss2[:, tc:tc + 1], AF.Sqrt, bias=epsc[:], scale=1.0 / D),
                   reads=[junk2B, constB], writes=[rbs[tc]])

            def fin_stage2(tc):
                s3 = tc % 3
                s2 = tc % 2
                rb = rbs[tc]
                op(DVE, lambda: nc.vector.reciprocal(rs2[:, tc:tc + 1], rs2[:, tc:tc + 1]),
                   reads=[rb], writes=[rb])
                op(DVE, lambda: nc.vector.scalar_tensor_tensor(yo[s2], xr[s3], rs2[:, tc:tc + 1], g2, ALU.mult, ALU.mult),
                   reads=[rb, xrB[s3], g2B], writes=[yoB[s2]])
                dma(SP, yosem[s2], lambda: nc.sync.dma_start(out=y[tc * 128:(tc + 1) * 128, :], in_=yo[s2]), reads=[yoB[s2]])
                if tc + 3 < 16:
                    load_xr(tc + 3)

            fin_stage1(0)
            for tc in range(16):
                if tc + 1 < 16:
                    fin_stage1(tc + 1)
                fin_stage2(tc)
            barrier()
    return nc


_NC_CACHE = {}


def kernel(x, meta_tokens, norm_g, w_in, conv_a_w, conv_a_b, ln_a_g, ln_a_b,
           w_a_out, b_a_out, conv_b_w, w_b_out, w_out, final_g):
    f = np.float32
    x = np.asarray(x, f)
    meta = np.asarray(meta_tokens, f)
    B = x.shape[0]
    ncores = 8
    pk = np.concatenate([np.asarray(conv_a_w, f)[0], np.asarray(conv_a_b, f), np.asarray(ln_a_g, f),
                         np.asarray(ln_a_b, f), np.asarray(b_a_out, f), np.asarray(conv_b_w, f)[0]], axis=0)
    assert pk.shape == (NCV, D)
    cvec = np.ascontiguousarray(pk.reshape(NCV, NJ, 128).transpose(2, 1, 0)).reshape(128, NJ * NCV)
    g1 = np.ascontiguousarray(np.broadcast_to(np.asarray(norm_g, f).reshape(1, D), (128, D)))
    g2 = np.ascontiguousarray(np.broadcast_to(np.asarray(final_g, f).reshape(1, D), (128, D)))
    ident = np.eye(128, dtype=f)
    wi = np.ascontiguousarray(np.asarray(w_in, f)[0])
    wa = np.ascontiguousarray(np.asarray(w_a_out, f)[0])
    wb = np.ascontiguousarray(np.asarray(w_b_out, f)[0])
    wo = np.ascontiguousarray(np.asarray(w_out, f)[0])
    in_maps = []
    for i in range(ncores):
        b, hf = i // 2, i % 2
        xmi = np.ascontiguousarray(x[b, hf * T:(hf + 1) * T])
        if hf == 0:
            xhi = np.concatenate([np.zeros((H - 16, D), f), meta], axis=0)
        else:
            xhi = np.ascontiguousarray(x[b, T - H:T])
        in_maps.append({"xm": xmi, "xh": xhi, "w_in": wi, "w_a": wa, "w_b": wb, "w_o": wo,
                        "cvec": cvec, "g1": g1, "g2": g2, "ident": ident})
    if "nc" not in _NC_CACHE:
        _NC_CACHE["nc"] = build_nc()
    nc = _NC_CACHE["nc"]
    res = run_bass_kernel_spmd(nc, in_maps, core_ids=list(range(ncores)))
    out = np.empty((B, 2 * T, D), f)
    for i in range(ncores):
        b, hf = i // 2, i % 2
        out[b, hf * T:(hf + 1) * T] = res.results[i]["y"]
    return out
```

```python
import numpy as np
from contextlib import ExitStack

import concourse.bass as bass
import concourse.mybir as mybir
from concourse.bass_utils import run_bass_kernel_spmd

F32 = mybir.dt.float32
BF16 = mybir.dt.bfloat16
AF = mybir.ActivationFunctionType
ALU = mybir.AluOpType

D = 1024
T = 2048
H = 32
TT = T + H
NJ = 8
KA = 31
KB = 3
EPS = 1e-6
NCV = 38
NWT = 6


class SemSrc:
    def __init__(self, nc, ctx, name, step):
        self.sem = ctx.enter_context(nc.semaphore(name))
        self.count = 0
        self.step = step
        self.name = name

    def mark(self, ins):
        ins.then_inc(self.sem, self.step)
        self.count += self.step
        return (self, self.count)


class Eng(SemSrc):
    def __init__(self, nc, ctx, e, name, kind):
        super().__init__(nc, ctx, "s_" + name, 1)
        self.e = e
        self.kind = kind
        self.waited = {}

    def wait(self, ev):
        src, val = ev
        if src is self:
            if self.kind == "pe":
                return
        if self.waited.get(src, 0) >= val:
            return
        self.e.wait_ge(src.sem, val)
        self.waited[src] = val


class Buf:
    def __init__(self, name=""):
        self.name = name
        self.w = None
        self.r = {}


def _deps(reads, writes):
    evs = []
    for b in reads:
        if b.w is not None:
            evs.append(b.w)
    for b in writes:
        if b.w is not None:
            evs.append(b.w)
        evs += list(b.r.items())
    return evs


def _commit(ev, reads, writes):
    for b in reads:
        b.r[ev[0]] = max(b.r.get(ev[0], 0), ev[1])
    for b in writes:
        b.w = ev
        b.r = {}


def op(eng, fn, reads=(), writes=()):
    for ev in _deps(reads, writes):
        eng.wait(ev)
    ins = fn()
    ev = eng.mark(ins)
    _commit(ev, reads, writes)
    return ev


def dma(qeng, dsem, fn, reads=(), writes=()):
    for ev in _deps(reads, writes):
        qeng.wait(ev)
    ins = fn()
    ev = dsem.mark(ins)
    _commit(ev, reads, writes)
    return ev


NP_TAPS = [13, 13, 13, 13, 13, 13, 18, 24]
NQ_TAPS = [0, 0, 0, 0, 0, 0, 0, 0]
NPMAX = max(NP_TAPS)
DVE_RATE = 1.2
POOL_RATE = 0.8
R_N, R_M, R_C, R_3 = 1.1, 1.1, 1.1, 1.1


def build_nc():
    nc = bass.Bass("TRN2", target_bir_lowering=False)
    xm = nc.dram_tensor("xm", [T, D], F32, kind="ExternalInput").ap()
    xh = nc.dram_tensor("xh", [H, D], F32, kind="ExternalInput").ap()
    w_in = nc.dram_tensor("w_in", [D, 9 * D], F32, kind="ExternalInput").ap()
    w_a = nc.dram_tensor("w_a", [D, D], F32, kind="ExternalInput").ap()
    w_b = nc.dram_tensor("w_b", [D, D], F32, kind="ExternalInput").ap()
    w_o = nc.dram_tensor("w_o", [D, D], F32, kind="ExternalInput").ap()
    cvec_d = nc.dram_tensor("cvec", [128, NJ * NCV], F32, kind="ExternalInput").ap()
    g1_d = nc.dram_tensor("g1", [128, D], F32, kind="ExternalInput").ap()
    g2_d = nc.dram_tensor("g2", [128, D], F32, kind="ExternalInput").ap()
    id_d = nc.dram_tensor("ident", [128, 128], F32, kind="ExternalInput").ap()
    y = nc.dram_tensor("y", [T, D], F32, kind="ExternalOutput").ap()

    w_in_v = w_in.rearrange("(k p) c -> p k c", p=128)
    w_a_v = w_a.rearrange("(k p) c -> p k c", p=128)
    w_b_v = w_b.rearrange("(k p) c -> p k c", p=128)
    w_o_v = w_o.rearrange("(k p) c -> p k c", p=128)

    with ExitStack() as ctx:
        PE = Eng(nc, ctx, nc.tensor, "pe", "pe")
        ACT = Eng(nc, ctx, nc.scalar, "act", "act")
        DVE = Eng(nc, ctx, nc.vector, "dve", "dve")
        POOL = Eng(nc, ctx, nc.gpsimd, "pool", "pool")
        SP = Eng(nc, ctx, nc.sync, "sp", "sp")
        engines = [PE, ACT, DVE, POOL, SP]
        dsems = []

        def new_dsem(name):
            s = SemSrc(nc, ctx, name, 16)
            dsems.append(s)
            return s

        def barrier():
            srcs = engines + dsems
            for e in engines:
                for s in srcs:
                    if s is e or s.count == 0:
                        continue
                    e.wait((s, s.count))

        def sb(c, name, shape, dt):
            return c.enter_context(nc.sbuf_tensor(name, shape, dt))

        hT = sb(ctx, "hT", [128, NJ, TT], BF16)
        UC = sb(ctx, "UC", [128, NJ, T], F32)
        UCb = UC[:].bitcast(BF16)
        VB = sb(ctx, "VB", [128, NJ, T], BF16)
        wt = [sb(ctx, f"wt{i}", [128, NJ, 128], BF16) for i in range(NWT)]
        cvec = sb(ctx, "cvec_s", [128, NJ, NCV], F32)
        idf = sb(ctx, "idf", [128, 128], F32)
        idb = sb(ctx, "idb", [128, 128], BF16)
        onesb = sb(ctx, "onesb", [128, 2], BF16)
        onesf = sb(ctx, "onesf", [128, 128], F32)
        ss1 = sb(ctx, "ss1", [128, 32], F32)
        rs1 = sb(ctx, "rs1", [128, 32], F32)
        st = sb(ctx, "st", [128, 32], F32)
        stw = sb(ctx, "stw", [128, 6, 16], F32)
        epsc = sb(ctx, "epsc", [128, 1], F32)

        NRING = 4
        ring = [ctx.enter_context(nc.psum_tensor(f"ring{i}", [128, 1024], F32)) for i in range(NRING)]
        ringB = [Buf(f"ring{i}") for i in range(NRING)]
        statsB = Buf("stats")
        ucnt = [0]

        def next_slot():
            s = ucnt[0] % NRING
            ucnt[0] += 1
            return ring[s], ringB[s]

        wtB = [Buf(f"wt{i}") for i in range(NWT)]
        wsem = [new_dsem(f"wsem{i}") for i in range(NWT)]
        csem = new_dsem("csem")
        hTB = Buf("hT")
        constB = Buf("const")

        tiles = []

        def G(g, j):
            return (w_in_v, g * D + j * 128)

        LEAD = 1
        for step in range(NJ + LEAD):
            if step < NJ:
                tiles += [G(1, step), G(0, step)]
            if step >= LEAD:
                j = step - LEAD
                tiles += [G(4, j), G(5, j), G(3, j), G(6, j)]
        for j in range(NJ):
            tiles += [G(2, j)]
        for i in range(NJ):
            tiles += [G(7, i), (w_a_v, i * 128), G(8, i), (w_b_v, i * 128)]
        tstate = {"issued": 0, "next_use": 0, "slot_users_left": [0] * NWT}

        def issue_tiles():
            while tstate["issued"] < len(tiles):
                i = tstate["issued"]
                s = i % NWT
                if tstate["slot_users_left"][s] != 0:
                    break
                view, c0 = tiles[i]
                dma(POOL, wsem[s],
                    lambda: nc.gpsimd.dma_start(out=wt[s][:], in_=view[:, :, c0:c0 + 128]),
                    writes=[wtB[s]])
                tstate["slot_users_left"][s] = -1
                tstate["issued"] += 1

        def acquire_tile(nusers):
            i = tstate["next_use"]
            tstate["next_use"] += 1
            assert i < tstate["issued"], "tile not issued"
            s = i % NWT
            tstate["slot_users_left"][s] = nusers
            return s

        def release_tile(s):
            tstate["slot_users_left"][s] -= 1
            if tstate["slot_users_left"][s] == 0:
                issue_tiles()

        c4 = new_dsem("c4")
        idB = Buf()
        dma(SP, csem, lambda: nc.sync.dma_start(out=cvec[:], in_=cvec_d.rearrange("p (j r) -> p j r", j=NJ)), writes=[constB])
        dma(SP, c4, lambda: nc.sync.dma_start(out=idf[:], in_=id_d), writes=[idB])
        issue_tiles()
        op(DVE, lambda: nc.vector.tensor_copy(idb[:], idf[:]), reads=[idB], writes=[constB])
        op(DVE, lambda: nc.vector.memset(onesb[:], 1.0), writes=[constB])
        op(DVE, lambda: nc.vector.memset(onesf[:], 1.0), writes=[constB])
        op(DVE, lambda: nc.vector.memset(ss1[:], 0.0), writes=[constB])
        op(DVE, lambda: nc.vector.memset(epsc[:], EPS), writes=[constB])
        op(DVE, lambda: nc.vector.memset(st[:], 0.0), writes=[statsB])

        with ExitStack() as pc:
            g1 = sb(pc, "g1_s", [128, D], F32)
            g1B = Buf()
            c2 = new_dsem("c2")
            dma(SP, c2, lambda: nc.sync.dma_start(out=g1[:], in_=g1_d), writes=[g1B])
            NXT = 6
            NHN = 4
            xt = [sb(pc, f"xt{i}", [128, D], F32) for i in range(NXT)]
            xtB = [Buf() for _ in range(NXT)]
            xsem = [new_dsem(f"xsem{i}") for i in range(NXT)]
            junk = sb(pc, "junk", [128, D], BF16)
            junkB = Buf()
            hn = [sb(pc, f"hn{i}", [128, D], BF16) for i in range(NHN)]
            hnB = [Buf() for _ in range(NHN)]

            def rows_of(c):
                return H if c == 0 else 128

            def stage_a(c):
                s3 = c % NXT
                sh = c % NHN
                rows = rows_of(c)
                src = xh if c == 0 else xm[(c - 1) * 128:c * 128, :]
                dma(SP, xsem[s3], lambda: nc.sync.dma_start(out=xt[s3][0:rows, :], in_=src), writes=[xtB[s3]])
                op(ACT, lambda: nc.scalar.activation(junk[0:rows, :], xt[s3][0:rows, :], AF.Square,
                                                     accum_out=ss1[0:rows, c:c + 1]),
                   reads=[xtB[s3], constB], writes=[junkB])
                rb = Buf()
                op(ACT, lambda: nc.scalar.activation(rs1[0:rows, c:c + 1], ss1[0:rows, c:c + 1], AF.Sqrt, bias=epsc[0:rows, :], scale=1.0 / D),
                   reads=[junkB, constB], writes=[rb])
                op(DVE, lambda: nc.vector.reciprocal(rs1[0:rows, c:c + 1], rs1[0:rows, c:c + 1]),
                   reads=[rb], writes=[rb])
                op(DVE, lambda: nc.vector.scalar_tensor_tensor(hn[sh][0:rows, :], xt[s3][0:rows, :], rs1[0:rows, c:c + 1],
                                                               g1[0:rows, :], ALU.mult, ALU.mult),
                   reads=[rb, xtB[s3], g1B], writes=[hnB[sh]])

            def stage_b(c):
                s3 = c % NHN
                rows = rows_of(c)
                tok0 = 0 if c == 0 else H + (c - 1) * 128
                ps, psB = next_slot()
                psb = ps[:].bitcast(BF16)

                def emit_tr():
                    last = None
                    for k in range(NJ):
                        last = nc.tensor.transpose(psb[:, k * 128:k * 128 + rows], hn[s3][0:rows, k * 128:(k + 1) * 128],
                                                   idb[0:rows, 0:rows])
                    return last
                op(PE, emit_tr, reads=[hnB[s3], constB], writes=[psB])
                src_ps = psb[:, 0:1024].rearrange("p (k t) -> p k t", k=NJ)[:, :, 0:rows]
                op(DVE, lambda: nc.vector.tensor_copy(hT[:, :, tok0:tok0 + rows], src_ps), reads=[psB], writes=[hTB])

            AHEAD = 3
            for c in range(AHEAD):
                stage_a(c)
            for c in range(17):
                if c + AHEAD < 17:
                    stage_a(c + AHEAD)
                stage_b(c)
            barrier()

        def main_unit(s, X, XB, tok0, extra_reads=()):
            ps, psB = next_slot()

            def emit():
                last = None
                for b in range(2):
                    for k in range(NJ):
                        last = nc.tensor.matmul(ps[:, b * 512:(b + 1) * 512], wt[s][:, k, :],
                                                X(k, tok0 + b * 512, 512), start=(k == 0), stop=(k == NJ - 1))
                return last
            op(PE, emit, reads=[wtB[s], XB] + list(extra_reads), writes=[psB])
            release_tile(s)
            return ps, psB

        def halo_unit(s):
            ps, psB = next_slot()

            def emit():
                last = None
                for k in range(NJ):
                    last = nc.tensor.matmul(ps[:, 0:H], wt[s][:, k, :], hT[:, k, 0:H], start=(k == 0), stop=(k == NJ - 1))
                return last
            op(PE, emit, reads=[wtB[s], hTB], writes=[psB])
            release_tile(s)
            return ps, psB

        def XhT(k, t0, n):
            return hT[:, k, H + t0:H + t0 + n]

        def Xva(k, t0, n):
            return UCb[:, k, t0:t0 + n]

        def Xvb(k, t0, n):
            return VB[:, k, t0:t0 + n]

        def conv_unit(dgt, dgB, taps, K, src, srcB, th):
            ps, psB = next_slot()

            def emit():
                last = None
                for b in range(2):
                    t0 = H + th * 1024 + b * 512 - (K - 1)
                    for i, k in enumerate(taps):
                        last = nc.tensor.matmul(ps[:, b * 512:(b + 1) * 512], dgt[:, i, :], src[:, t0 + k:t0 + k + 512],
                                                start=(i == 0), stop=(i == len(taps) - 1))
                return last
            op(PE, emit, reads=[dgB, srcB], writes=[psB])
            return ps, psB

        ucB = [Buf(f"uc{j}") for j in range(NJ)]
        vaB = Buf("va")
        vbB = Buf("vb")

        with ExitStack() as pc:
            NUA = 3
            ua = [sb(pc, f"ua{i}", [128, TT], BF16) for i in range(NUA)]
            uaB = [Buf(f"ua{i}") for i in range(NUA)]
            sg = [sb(pc, f"sg{i}", [128, 1024], F32) for i in range(2)]
            sgB = [Buf() for _ in range(2)]
            sgh = sb(pc, "sgh", [128, H], F32)
            sghB = Buf()
            hv = sb(pc, "hv", [128, H], F32)
            hvB = Buf()
            stt = sb(pc, "stt", [128, 16], F32)
            sttB = Buf()
            dg = sb(pc, "dg", [128, NPMAX, 128], BF16)
            dgB = Buf()
            ucb = sb(pc, "ucb", [128, 1024], BF16)
            ucq = sb(pc, "ucq", [128, 1024], BF16)
            ucbB, ucqB = Buf(), Buf()
            NEV = 3
            evt = [sb(pc, f"ev{i}", [128, 1024], F32) for i in range(NEV)]
            evB = [Buf() for _ in range(NEV)]
            evc = [0]

            def next_ev():
                e = evc[0] % NEV
                evc[0] += 1
                return evt[e], evB[e]
            bc = [sb(pc, f"bc{i}", [128, 1024], F32) for i in range(2)]
            bch = sb(pc, "bch", [128, H], F32)
            sbz = [sb(pc, f"sbz{i}", [128, 1024], F32) for i in range(2)]
            cxb = sb(pc, "cxb", [128, TT], BF16)
            dg3 = sb(pc, "dg3", [128, KB, 128], BF16)
            bcB = [Buf() for _ in range(2)]
            bchB = Buf()
            sbzB = [Buf() for _ in range(2)]
            cxbB = Buf()
            dg3B = Buf()

            dve_bg = []
            pool_bg = []

            credit = [0.0, 0.0]
            lag_q = [[], []]

            def defer(fn):
                lag_q[1].append(fn)

            timed = []
            stats_ready = []

            def after_pumps(n, fn):
                timed.append([n, fn])

            def pump(force=False, rate=None):
                for t_ in timed:
                    t_[0] -= 1
                due = [t_ for t_ in timed if t_[0] <= 0]
                for t_ in due:
                    timed.remove(t_)
                    t_[1]()
                for fn in lag_q[0]:
                    fn()
                lag_q[0] = lag_q[1]
                lag_q[1] = []
                credit[0] += DVE_RATE if rate is None else rate
                credit[1] += POOL_RATE
                while dve_bg and (force or credit[0] >= 1.0):
                    credit[0] -= 1.0
                    dve_bg.pop(0)()
                    if force:
                        break
                while pool_bg and (force or credit[1] >= 1.0):
                    credit[1] -= 1.0
                    pool_bg.pop(0)()
                    if force:
                        break

            def tap_op(eng, e, j, k):
                u = ua[j % NUA]
                uBf = uaB[j % NUA]
                src = u[:, H - (KA - 1) + k:H - (KA - 1) + k + T]
                op(eng, lambda: e.scalar_tensor_tensor(UC[:, j, :], src, cvec[:, j, k:k + 1], UC[:, j, :], ALU.mult, ALU.add),
                   reads=[uBf, constB, ucB[j]], writes=[ucB[j]])

            def pool_tap(j, k, th):
                u = ua[j % NUA]
                uBf = uaB[j % NUA]
                o = H - (KA - 1) + k + th * 1024
                op(POOL, lambda: nc.gpsimd.tensor_scalar(tmpP[:], u[:, o:o + 1024], cvec[:, j, k:k + 1], None, ALU.mult),
                   reads=[uBf, constB], writes=[tmpPB])
                op(POOL, lambda: nc.gpsimd.tensor_tensor(UC[:, j, th * 1024:(th + 1) * 1024], UC[:, j, th * 1024:(th + 1) * 1024], tmpP[:], ALU.add),
                   reads=[tmpPB, ucB[j]], writes=[ucB[j]])

            def sched_conv(j):
                nP, nQ = NP_TAPS[j], NQ_TAPS[j]
                dtaps = list(range(nP, KA - nQ))
                qtaps = list(range(KA - nQ, KA))

                def enqueue_pool():
                    for k in qtaps:
                        for th in range(2):
                            pool_bg.append(lambda k=k, th=th: pool_tap(j, k, th))
                for idx, k in enumerate(dtaps):
                    dve_bg.append(lambda k=k: tap_op(DVE, nc.vector, j, k))
                if qtaps:
                    dve_bg.append(enqueue_pool)
                dve_bg.append(lambda: stats_ready.append(j))

            def stats_act(j, th):
                tsl = slice(th * 1024, (th + 1) * 1024)
                op(ACT, lambda: nc.scalar.activation(ucq[:], UC[:, j, tsl], AF.Square), reads=[ucB[j]], writes=[ucqB])
                op(ACT, lambda: nc.scalar.copy(ucb[:], UC[:, j, tsl]), reads=[ucB[j]], writes=[ucbB])

            def stats_pe(j, th):
                ps, psB = next_slot()

                def emit_st():
                    last = None
                    for c in range(8):
                        nc.tensor.matmul(ps[:, 2 * c:2 * c + 1], ucb[:, c * 128:(c + 1) * 128], onesb[:, 0:1], start=True, stop=True)
                        last = nc.tensor.matmul(ps[:, 2 * c + 1:2 * c + 2], ucq[:, c * 128:(c + 1) * 128], onesb[:, 0:1], start=True, stop=True)
                    return last
                op(PE, emit_st, reads=[ucbB, ucqB, constB], writes=[psB])
                op(ACT, lambda: nc.scalar.copy(stt[:], ps[:, 0:16]), reads=[psB], writes=[sttB])
                op(DVE, lambda: nc.vector.tensor_tensor(st[:, th * 16:(th + 1) * 16], stt[:], st[:, th * 16:(th + 1) * 16], ALU.add),
                   reads=[sttB, statsB], writes=[statsB])

            def sched_stats(j):
                def mid():
                    stats_pe(j, 0)
                    stats_act(j, 1)
                after_pumps(4, lambda: stats_act(j, 0))
                after_pumps(8, mid)
                after_pumps(12, lambda: stats_pe(j, 1))

            def emit_A1(j):
                u = ua[j % NUA]
                uBf = uaB[j % NUA]
                nP = NP_TAPS[j]
                sG = acquire_tile(3)
                sV = acquire_tile(3)
                ps, psB = halo_unit(sG)
                op(ACT, lambda: nc.scalar.activation(sgh[:], ps[:, 0:H], AF.Sigmoid), reads=[psB], writes=[sghB])
                ps, psB = halo_unit(sV)
                op(ACT, lambda: nc.scalar.copy(hv[:], ps[:, 0:H]), reads=[psB], writes=[hvB])
                op(DVE, lambda: nc.vector.tensor_tensor(u[:, 0:H], hv[:], sgh[:], ALU.mult),
                   reads=[hvB, sghB], writes=[uBf])
                for th in range(2):
                    ps, psB = main_unit(sG, XhT, hTB, th * 1024)
                    op(ACT, lambda: nc.scalar.activation(sg[th][:], ps[:], AF.Sigmoid), reads=[psB], writes=[sgB[th]])
                    pump(rate=R_N)
                    ps, psB = main_unit(sV, XhT, hTB, th * 1024)
                    ev, eB = next_ev()
                    op(ACT, lambda: nc.scalar.copy(ev[:], ps[:]), reads=[psB], writes=[eB])
                    defer(lambda ev=ev, eB=eB, th=th: op(
                        DVE, lambda: nc.vector.tensor_tensor(u[:, H + th * 1024:H + (th + 1) * 1024], ev[:], sg[th][:], ALU.mult),
                        reads=[eB, sgB[th]], writes=[uBf]))
                    pump(rate=R_M)

            def build_dg(j):
                nP = NP_TAPS[j]
                in0 = bass.AP(idb, 0, [[128, 128], [0, nP], [1, 128]])
                in1 = bass.AP(cvec, j * NCV, [[NJ * NCV, 128], [1, nP], [0, 128]])
                op(DVE, lambda: nc.vector.tensor_tensor(dg[:, 0:nP, :], in0, in1, ALU.mult), reads=[constB], writes=[dgB])

            def emit_A1conv(j):
                u = ua[j % NUA]
                uBf = uaB[j % NUA]
                nP = NP_TAPS[j]
                bias = cvec[:, j, 31:32]
                for th in range(2):
                    ps, psB = conv_unit(dg, dgB, list(range(nP)), KA, u, uBf, th)
                    op(ACT, lambda: nc.scalar.activation(UC[:, j, th * 1024:(th + 1) * 1024], ps[:], AF.Identity, bias=bias),
                       reads=[psB, constB], writes=[ucB[j]])
                    pump(rate=R_C)
                sched_conv(j)

            def emit_B(j, mid_hook=None):
                if stats_ready and not timed:
                    sched_stats(stats_ready.pop(0))
                in0 = bass.AP(idb, 0, [[128, 128], [0, KB], [1, 128]])
                in1 = bass.AP(cvec, j * NCV + 35, [[NJ * NCV, 128], [1, KB], [0, 128]])
                op(DVE, lambda: nc.vector.tensor_tensor(dg3[:], in0, in1, ALU.mult), reads=[constB], writes=[dg3B])
                sC = acquire_tile(3)
                sX = acquire_tile(3)
                sBt = acquire_tile(2)
                sZ = acquire_tile(2)
                ps, psB = halo_unit(sC)
                op(ACT, lambda: nc.scalar.copy(bch[:], ps[:, 0:H]), reads=[psB], writes=[bchB])
                ps, psB = halo_unit(sX)
                op(ACT, lambda: nc.scalar.copy(hv[:], ps[:, 0:H]), reads=[psB], writes=[hvB])
                op(DVE, lambda: nc.vector.tensor_tensor(cxb[:, 0:H], hv[:], bch[:], ALU.mult),
                   reads=[hvB, bchB], writes=[cxbB])
                for th in range(2):
                    ps, psB = main_unit(sC, XhT, hTB, th * 1024)
                    op(ACT, lambda: nc.scalar.copy(bc[th][:], ps[:]), reads=[psB], writes=[bcB[th]])
                    pump(rate=R_N)
                    ps, psB = main_unit(sX, XhT, hTB, th * 1024)
                    ev, eB = next_ev()
                    op(ACT, lambda: nc.scalar.copy(ev[:], ps[:]), reads=[psB], writes=[eB])
                    op(DVE, lambda: nc.vector.tensor_tensor(cxb[:, H + th * 1024:H + (th + 1) * 1024], ev[:], bc[th][:], ALU.mult),
                       reads=[eB, bcB[th]], writes=[cxbB])
                    pump(rate=R_M)
                if mid_hook is not None:
                    mid_hook()
                for th in range(2):
                    ps, psB = main_unit(sBt, XhT, hTB, th * 1024)
                    ev, eB = next_ev()
                    op(ACT, lambda: nc.scalar.copy(ev[:], ps[:]), reads=[psB], writes=[eB])
                    pump(rate=R_N)
                    ps, psB = main_unit(sZ, XhT, hTB, th * 1024)
                    op(ACT, lambda: nc.scalar.activation(sbz[th][:], ps[:], AF.Silu), reads=[psB], writes=[sbzB[th]])
                    defer(lambda ev=ev, eB=eB, th=th: op(
                        DVE, lambda: nc.vector.tensor_tensor(sbz[th][:], ev[:], sbz[th][:], ALU.mult),
                        reads=[eB, sbzB[th]], writes=[sbzB[th]]))
                    pump(rate=R_M)
                for th in range(2):
                    ps, psB = conv_unit(dg3, dg3B, list(range(KB)), KB, cxb, cxbB, th)
                    ev, eB = next_ev()
                    op(ACT, lambda: nc.scalar.copy(ev[:], ps[:]), reads=[psB], writes=[eB])
                    defer(lambda ev=ev, eB=eB, th=th: op(
                        DVE, lambda: nc.vector.tensor_tensor(VB[:, j, th * 1024:(th + 1) * 1024], ev[:], sbz[th][:], ALU.mult),
                        reads=[eB, sbzB[th]], writes=[vbB]))
                    pump(rate=R_3)

            for step in range(NJ + LEAD):
                if stats_ready and not timed:
                    sched_stats(stats_ready.pop(0))
                if step < NJ:
                    emit_A1(step)
                    if step == LEAD:
                        for j0 in range(LEAD):
                            build_dg(j0)
                            emit_A1conv(j0)
                if step >= LEAD:
                    if step < NJ:
                        build_dg(step)
                    emit_B(step - LEAD, mid_hook=(lambda st_=step: emit_A1conv(st_)) if step < NJ else None)
            while dve_bg or pool_bg or lag_q[0] or lag_q[1] or timed:
                pump(force=True)
            while stats_ready:
                j_ = stats_ready.pop(0)
                for th_ in range(2):
                    stats_act(j_, th_)
                    stats_pe(j_, th_)
            barrier()

        with ExitStack() as pc:
            rbc = sb(pc, "rbc", [128, T], F32)
            nbc = sb(pc, "nbc", [128, T], F32)
            rbcB, nbcB = Buf(), Buf()
            zt = [sb(pc, f"zt{i}", [128, 1024], F32) for i in range(2)]
            sz = [sb(pc, f"sz{i}", [128, 1024], F32) for i in range(2)]
            ztB = [Buf() for _ in range(2)]
            szB = [Buf() for _ in range(2)]
            rhsR = sb(pc, "rhsR", [128, T], F32)
            rhsN = sb(pc, "rhsN", [128, T], F32)
            units = [(j, th) for j in range(NJ) for th in range(2)]
            tile_of = {}

            done1a = set()

            def a2_stage1a(u):
                if u in done1a:
                    return
                done1a.add(u)
                j, th = units[u]
                q = u % 2
                if th == 0:
                    tile_of[j] = acquire_tile(2)
                ps, psB = main_unit(tile_of[j], XhT, hTB, th * 1024)
                op(ACT, lambda: nc.scalar.activation(sz[q][:], ps[:], AF.Silu), reads=[psB], writes=[szB[q]])

            def a2_stage1(u):
                a2_stage1a(u)
                j, th = units[u]
                q = u % 2
                tsl = slice(th * 1024, (th + 1) * 1024)
                op(DVE, lambda: nc.vector.tensor_tensor(zt[q][:], UC[:, j, tsl], rbc[:, tsl], ALU.mult),
                   reads=[ucB[j], rbcB], writes=[ztB[q]])
                op(DVE, lambda: nc.vector.tensor_tensor(zt[q][:], zt[q][:], nbc[:, tsl], ALU.add),
                   reads=[nbcB, ztB[q]], writes=[ztB[q]])
                op(ACT, lambda: nc.scalar.activation(zt[q][:], zt[q][:], AF.Silu, bias=cvec[:, j, 33:34], scale=cvec[:, j, 32:33]),
                   reads=[ztB[q], constB], writes=[ztB[q]])

            def a2_stage2(u):
                j, th = units[u]
                q = u % 2
                op(DVE, lambda: nc.vector.tensor_tensor(UCb[:, j, th * 1024:(th + 1) * 1024], zt[q][:], sz[q][:], ALU.mult),
                   reads=[ztB[q], szB[q], ucB[j]], writes=[vaB, ucB[j]])

            stB = statsB
            st3 = st[:].rearrange("p (c q) -> p c q", q=2)
            mu, msq, var, rstd, nmr = (stw[:, i, :] for i in range(5))
            op(DVE, lambda: nc.vector.tensor_scalar(mu, st3[:, :, 0], 1.0 / D, None, ALU.mult), reads=[stB], writes=[stB])
            op(DVE, lambda: nc.vector.tensor_tensor(msq, mu, mu, ALU.mult), reads=[stB], writes=[stB])
            op(DVE, lambda: nc.vector.scalar_tensor_tensor(var, st3[:, :, 1], 1.0 / D, msq, ALU.mult, ALU.subtract), reads=[stB], writes=[stB])
            op(ACT, lambda: nc.scalar.activation(rstd, var, AF.Sqrt, bias=epsc[:], scale=1.0), reads=[stB, constB], writes=[stB])
            op(DVE, lambda: nc.vector.reciprocal(rstd, rstd), reads=[stB], writes=[stB])
            op(DVE, lambda: nc.vector.scalar_tensor_tensor(nmr, mu, -1.0, rstd, ALU.mult, ALU.mult), reads=[stB], writes=[stB])
            rB, nB = Buf(), Buf()
            idbc = bass.AP(idf, 0, [[128, 128], [0, 16], [1, 128]])
            op(DVE, lambda: nc.vector.tensor_tensor(rhsR[:].rearrange("p (c t) -> p c t", c=16), idbc,
                                                    bass.AP(stw, 3 * 16, [[96, 128], [1, 16], [0, 128]]), ALU.mult),
               reads=[stB, idB], writes=[rB])
            op(DVE, lambda: nc.vector.tensor_tensor(rhsN[:].rearrange("p (c t) -> p c t", c=16), idbc,
                                                    bass.AP(stw, 4 * 16, [[96, 128], [1, 16], [0, 128]]), ALU.mult),
               reads=[stB, idB], writes=[nB])
            a2_stage1a(0)
            a2_stage1a(1)
            for (srcT, srcB, dst, dstB) in ((rhsR, rB, rbc, rbcB), (rhsN, nB, nbc, nbcB)):
                for th in range(2):
                    ps, psB = next_slot()

                    def emit_bc():
                        last = None
                        for b in range(2):
                            last = nc.tensor.matmul(ps[:, b * 512:(b + 1) * 512], onesf[:], srcT[:, th * 1024 + b * 512:th * 1024 + (b + 1) * 512],
                                                    start=True, stop=True)
                        return last
                    op(PE, emit_bc, reads=[srcB, constB], writes=[psB])
                    op(ACT, lambda: nc.scalar.copy(dst[:, th * 1024:(th + 1) * 1024], ps[:]), reads=[psB], writes=[dstB])
            a2_stage1(0)
            for u in range(len(units)):
                if u + 1 < len(units):
                    a2_stage1(u + 1)
                a2_stage2(u)

            mB = Buf("m")
            wo = sb(pc, "wo", [128, NJ, D], BF16)
            woB = Buf()
            wosem = new_dsem("wosem")
            if True:
                sga = [zt[0][:], zt[1][:]]
                sgaB = ztB
                sgb = [sz[0][:], sz[1][:]]
                sgbB = szB
                t1 = [rhsR[:, 0:1024], rhsR[:, 1024:2048]]
                t1B = [Buf() for _ in range(2)]
                for b_ in t1B:
                    b_.w = rB.w
                    b_.r = dict(rB.r)
                mBs = [Buf("m0"), Buf("m1")]

                NXR = 4
                g2 = rhsR[:, 1024:2048]
                g2B = Buf()
                c3 = new_dsem("c3")
                xr = [rbc[:, 0:1024], rbc[:, 1024:2048], nbc[:, 0:1024], nbc[:, 1024:2048]]
                xrB = [Buf() for _ in range(NXR)]
                for b_, src_ in zip(xrB, (rbcB, rbcB, nbcB, nbcB)):
                    b_.w, b_.r = src_.w, dict(src_.r)
                xrsem = [new_dsem(f"xrsem{i}") for i in range(NXR)]
                yo = [rhsN[:, 0:1024], rhsN[:, 1024:2048]]
                yoB = [Buf() for _ in range(2)]
                for b_ in yoB:
                    b_.w, b_.r = nB.w, dict(nB.r)
                yosem = [new_dsem(f"yosem{i}") for i in range(2)]

                def load_xr(t_):
                    s_ = t_ % NXR
                    dma(SP, xrsem[s_], lambda: nc.sync.dma_start(out=xr[s_], in_=xm[t_ * 128:(t_ + 1) * 128, :]), writes=[xrB[s_]])

                def u_ga(i, th, q, sGa):
                    ps, psB = main_unit(sGa, XhT, hTB, th * 1024)
                    op(ACT, lambda: nc.scalar.activation(sga[q], ps[:], AF.Sigmoid), reads=[psB], writes=[sgaB[q]])

                def u_ya(i, th, q, sWa):
                    ps, psB = main_unit(sWa, Xva, vaB, th * 1024)
                    op(DVE, lambda: nc.vector.scalar_tensor_tensor(t1[q], ps[:], cvec[:, i, 34:35], sga[q], ALU.add, ALU.mult),
                       reads=[psB, sgaB[q], constB], writes=[t1B[q]])

                def u_gb(i, th, q, sGb):
                    ps, psB = main_unit(sGb, XhT, hTB, th * 1024)
                    op(ACT, lambda: nc.scalar.activation(sgb[q], ps[:], AF.Sigmoid), reads=[psB], writes=[sgbB[q]])

                def u_yb(i, th, q, sWb):
                    ps, psB = main_unit(sWb, Xvb, vbB, th * 1024)
                    op(DVE, lambda: nc.vector.tensor_tensor(sgb[q], ps[:], sgb[q], ALU.mult),
                       reads=[psB, sgbB[q]], writes=[sgbB[q]])

                def u_m(i, th, q):
                    op(DVE, lambda: nc.vector.tensor_tensor(UCb[:, i, T + th * 1024:T + (th + 1) * 1024], t1[q], sgb[q], ALU.add),
                       reads=[t1B[q], sgbB[q]], writes=[mBs[th]])

                for i in range(NJ):
                    sGa = acquire_tile(2)
                    sWa = acquire_tile(2)
                    sGb = acquire_tile(2)
                    sWb = acquire_tile(2)
                    if i == NJ - 1:
                        dma(POOL, wosem, lambda: nc.gpsimd.dma_start(out=wo[:], in_=w_o_v), writes=[woB])
                        for t_ in range(NXR):
                            load_xr(t_)
                    if i == 0:
                        for th in range(2):
                            u_ga(i, th, th, sGa)
                        for th in range(2):
                            u_gb(i, th, th, sGb)
                        for th in range(2):
                            u_yb(i, th, th, sWb)
                        for th in range(2):
                            u_ya(i, th, th, sWa)
                            u_m(i, th, th)
                    else:
                        for th in range(2):
                            q = th
                            u_ga(i, th, q, sGa)
                            u_ya(i, th, q, sWa)
                            u_gb(i, th, q, sGb)
                            u_yb(i, th, q, sWb)
                            u_m(i, th, q)
            g2B.w, g2B.r = t1B[1].w, dict(t1B[1].r)
            dma(SP, c3, lambda: nc.sync.dma_start(out=g2, in_=g2_d), writes=[g2B])
            junk2 = rhsR[:].bitcast(BF16)[:, 0:D]
            junk2B = Buf()
            junk2B.w, junk2B.r = t1B[0].w, dict(t1B[0].r)
            ss2 = ss1[:, 0:16]
            rs2 = rs1[:, 0:16]
            ss2B = Buf()
            op(DVE, lambda: nc.vector.memset(ss2, 0.0), writes=[ss2B])
            rbs = [Buf() for _ in range(16)]

            def fin_stage1(tc):
                s3 = tc % NXR
                ps, psB = next_slot()

                def emit_f():
                    last = None
                    for n in range(2):
                        for k in range(NJ):
                            last = nc.tensor.matmul(ps[:, n * 512:(n + 1) * 512], UCb[:, k, T + tc * 128:T + (tc + 1) * 128],
                                                    wo[:, k, n * 512:(n + 1) * 512], start=(k == 0), stop=(k == NJ - 1))
                    return last
                op(PE, emit_f, reads=[mBs[tc // 8], woB], writes=[psB])
                op(DVE, lambda: nc.vector.tensor_tensor(xr[s3], ps[:], xr[s3], ALU.add), reads=[psB, xrB[s3]], writes=[xrB[s3]])
                op(ACT, lambda: nc.scalar.activation(junk2, xr[s3], AF.Square, accum_out=ss2[:, tc:tc + 1]),
                   reads=[xrB[s3], ss2B], writes=[junk2B])
                op(ACT, lambda: nc.scalar.activation(rs2[:, tc:tc + 1], ss2[:, tc:tc + 1], AF.Sqrt, bias=epsc[:], scale=1.0 / D),
                   reads=[junk2B, constB], writes=[rbs[tc]])

            def fin_stage2(tc):
                s3 = tc % NXR
                s2 = tc % 2
                rb = rbs[tc]
                op(DVE, lambda: nc.vector.reciprocal(rs2[:, tc:tc + 1], rs2[:, tc:tc + 1]),
                   reads=[rb], writes=[rb])
                op(DVE, lambda: nc.vector.scalar_tensor_tensor(yo[s2], xr[s3], rs2[:, tc:tc + 1], g2, ALU.mult, ALU.mult),
                   reads=[rb, xrB[s3], g2B], writes=[yoB[s2]])
                dma(SP, yosem[s2], lambda: nc.sync.dma_start(out=y[tc * 128:(tc + 1) * 128, :], in_=yo[s2]), reads=[yoB[s2]])
                if tc + NXR < 16:
                    load_xr(tc + NXR)

            fin_stage1(0)
            for tc in range(16):
                if tc + 1 < 16:
                    fin_stage1(tc + 1)
                fin_stage2(tc)
            barrier()
    return nc


_NC_CACHE = {}


def kernel(x, meta_tokens, norm_g, w_in, conv_a_w, conv_a_b, ln_a_g, ln_a_b,
           w_a_out, b_a_out, conv_b_w, w_b_out, w_out, final_g):
    f = np.float32
    x = np.asarray(x, f)
    meta = np.asarray(meta_tokens, f)
    B = x.shape[0]
    ncores = 8
    pk = np.concatenate([np.asarray(conv_a_w, f)[0], np.asarray(conv_a_b, f), np.asarray(ln_a_g, f),
                         np.asarray(ln_a_b, f), np.asarray(b_a_out, f), np.asarray(conv_b_w, f)[0]], axis=0)
    assert pk.shape == (NCV, D)
    cvec = np.ascontiguousarray(pk.reshape(NCV, NJ, 128).transpose(2, 1, 0)).reshape(128, NJ * NCV)
    g1 = np.ascontiguousarray(np.broadcast_to(np.asarray(norm_g, f).reshape(1, D), (128, D)))
    g2 = np.ascontiguousarray(np.broadcast_to(np.asarray(final_g, f).reshape(1, D), (128, D)))
    ident = np.eye(128, dtype=f)
    wi = np.ascontiguousarray(np.asarray(w_in, f)[0])
    wa = np.ascontiguousarray(np.asarray(w_a_out, f)[0])
    wb = np.ascontiguousarray(np.asarray(w_b_out, f)[0])
    wo = np.ascontiguousarray(np.asarray(w_out, f)[0])
    in_maps = []
    for i in range(ncores):
        b, hf = i // 2, i % 2
        xmi = np.ascontiguousarray(x[b, hf * T:(hf + 1) * T])
        if hf == 0:
            xhi = np.concatenate([np.zeros((H - 16, D), f), meta], axis=0)
        else:
            xhi = np.ascontiguousarray(x[b, T - H:T])
        in_maps.append({"xm": xmi, "xh": xhi, "w_in": wi, "w_a": wa, "w_b": wb, "w_o": wo,
                        "cvec": cvec, "g1": g1, "g2": g2, "ident": ident})
    if "nc" not in _NC_CACHE:
        _NC_CACHE["nc"] = build_nc()
    nc = _NC_CACHE["nc"]
    res = run_bass_kernel_spmd(nc, in_maps, core_ids=list(range(ncores)))
    out = np.empty((B, 2 * T, D), f)
    for i in range(ncores):
        b, hf = i // 2, i % 2
        out[b, hf * T:(hf + 1) * T] = res.results[i]["y"]
    return out
```

```python
import numpy as np
from contextlib import ExitStack

import concourse.bass as bass
import concourse.mybir as mybir
from concourse.bass_utils import run_bass_kernel_spmd

F32 = mybir.dt.float32
BF16 = mybir.dt.bfloat16
AF = mybir.ActivationFunctionType
ALU = mybir.AluOpType

D = 1024
T = 2048
H = 32
TT = T + H
NJ = 8
KA = 31
KB = 3
EPS = 1e-6
NCV = 38
NWT = 6


class SemSrc:
    def __init__(self, nc, ctx, name, step):
        self.sem = ctx.enter_context(nc.semaphore(name))
        self.count = 0
        self.step = step
        self.name = name

    def mark(self, ins):
        ins.then_inc(self.sem, self.step)
        self.count += self.step
        return (self, self.count)


class Eng(SemSrc):
    def __init__(self, nc, ctx, e, name, kind):
        super().__init__(nc, ctx, "s_" + name, 1)
        self.e = e
        self.kind = kind
        self.waited = {}

    def wait(self, ev):
        src, val = ev
        if src is self:
            if self.kind == "pe":
                return
        if self.waited.get(src, 0) >= val:
            return
        self.e.wait_ge(src.sem, val)
        self.waited[src] = val


class Buf:
    def __init__(self, name=""):
        self.name = name
        self.w = None
        self.r = {}


def _deps(reads, writes):
    evs = []
    for b in reads:
        if b.w is not None:
            evs.append(b.w)
    for b in writes:
        if b.w is not None:
            evs.append(b.w)
        evs += list(b.r.items())
    return evs


def _commit(ev, reads, writes):
    for b in reads:
        b.r[ev[0]] = max(b.r.get(ev[0], 0), ev[1])
    for b in writes:
        b.w = ev
        b.r = {}


def op(eng, fn, reads=(), writes=()):
    for ev in _deps(reads, writes):
        eng.wait(ev)
    ins = fn()
    ev = eng.mark(ins)
    _commit(ev, reads, writes)
    return ev


def dma(qeng, dsem, fn, reads=(), writes=()):
    for ev in _deps(reads, writes):
        qeng.wait(ev)
    ins = fn()
    ev = dsem.mark(ins)
    _commit(ev, reads, writes)
    return ev


NP_TAPS = [13, 13, 13, 13, 13, 13, 18, 24]
NQ_TAPS = [0, 0, 0, 0, 0, 0, 0, 0]
NPMAX = max(NP_TAPS)
DVE_RATE = 1.2
POOL_RATE = 0.8
R_N, R_M, R_C, R_3 = 1.1, 1.1, 1.1, 1.1


def build_nc():
    nc = bass.Bass("TRN2", target_bir_lowering=False)
    xm = nc.dram_tensor("xm", [T, D], F32, kind="ExternalInput").ap()
    xh = nc.dram_tensor("xh", [H, D], F32, kind="ExternalInput").ap()
    w_in = nc.dram_tensor("w_in", [D, 9 * D], F32, kind="ExternalInput").ap()
    w_a = nc.dram_tensor("w_a", [D, D], F32, kind="ExternalInput").ap()
    w_b = nc.dram_tensor("w_b", [D, D], F32, kind="ExternalInput").ap()
    w_o = nc.dram_tensor("w_o", [D, D], F32, kind="ExternalInput").ap()
    cvec_d = nc.dram_tensor("cvec", [128, NJ * NCV], F32, kind="ExternalInput").ap()
    g1_d = nc.dram_tensor("g1", [128, D], F32, kind="ExternalInput").ap()
    g2_d = nc.dram_tensor("g2", [128, D], F32, kind="ExternalInput").ap()
    id_d = nc.dram_tensor("ident", [128, 128], F32, kind="ExternalInput").ap()
    y = nc.dram_tensor("y", [T, D], F32, kind="ExternalOutput").ap()

    w_in_v = w_in.rearrange("(k p) c -> p k c", p=128)
    w_a_v = w_a.rearrange("(k p) c -> p k c", p=128)
    w_b_v = w_b.rearrange("(k p) c -> p k c", p=128)
    w_o_v = w_o.rearrange("(k p) c -> p k c", p=128)

    with ExitStack() as ctx:
        PE = Eng(nc, ctx, nc.tensor, "pe", "pe")
        ACT = Eng(nc, ctx, nc.scalar, "act", "act")
        DVE = Eng(nc, ctx, nc.vector, "dve", "dve")
        POOL = Eng(nc, ctx, nc.gpsimd, "pool", "pool")
        SP = Eng(nc, ctx, nc.sync, "sp", "sp")
        engines = [PE, ACT, DVE, POOL, SP]
        dsems = []

        def new_dsem(name):
            s = SemSrc(nc, ctx, name, 16)
            dsems.append(s)
            return s

        def barrier():
            srcs = engines + dsems
            for e in engines:
                for s in srcs:
                    if s is e or s.count == 0:
                        continue
                    e.wait((s, s.count))

        def sb(c, name, shape, dt):
            return c.enter_context(nc.sbuf_tensor(name, shape, dt))

        hT = sb(ctx, "hT", [128, NJ, TT], BF16)
        UC = sb(ctx, "UC", [128, NJ, T], F32)
        UCb = UC[:].bitcast(BF16)
        VB = sb(ctx, "VB", [128, NJ, T], BF16)
        wt = [sb(ctx, f"wt{i}", [128, NJ, 128], BF16) for i in range(NWT)]
        cvec = sb(ctx, "cvec_s", [128, NJ, NCV], F32)
        idf = sb(ctx, "idf", [128, 128], F32)
        idb = sb(ctx, "idb", [128, 128], BF16)
        onesb = sb(ctx, "onesb", [128, 2], BF16)
        onesf = sb(ctx, "onesf", [128, 128], F32)
        ss1 = sb(ctx, "ss1", [128, 32], F32)
        rs1 = sb(ctx, "rs1", [128, 32], F32)
        st = sb(ctx, "st", [128, 32], F32)
        stw = sb(ctx, "stw", [128, 6, 16], F32)
        epsc = sb(ctx, "epsc", [128, 1], F32)

        NRING = 4
        ring = [ctx.enter_context(nc.psum_tensor(f"ring{i}", [128, 1024], F32)) for i in range(NRING)]
        ringB = [Buf(f"ring{i}") for i in range(NRING)]
        statsB = Buf("stats")
        ucnt = [0]

        def next_slot():
            s = ucnt[0] % NRING
            ucnt[0] += 1
            return ring[s], ringB[s]

        wtB = [Buf(f"wt{i}") for i in range(NWT)]
        wsem = [new_dsem(f"wsem{i}") for i in range(NWT)]
        csem = new_dsem("csem")
        hTB = Buf("hT")
        constB = Buf("const")

        tiles = []

        def G(g, j):
            return (w_in_v, g * D + j * 128)

        LEAD = 1
        for step in range(NJ + LEAD):
            if step < NJ:
                tiles += [G(1, step), G(0, step)]
            if step >= LEAD:
                j = step - LEAD
                tiles += [G(4, j), G(5, j), G(3, j), G(6, j)]
        for j in range(NJ):
            tiles += [G(2, j)]
        for i in range(NJ):
            tiles += [G(7, i), (w_a_v, i * 128), G(8, i), (w_b_v, i * 128)]
        tstate = {"issued": 0, "next_use": 0, "slot_users_left": [0] * NWT}

        tile_gate = []

        def issue_tiles():
            while tstate["issued"] < len(tiles):
                i = tstate["issued"]
                s = i % NWT
                if tstate["slot_users_left"][s] != 0:
                    break
                view, c0 = tiles[i]
                dma(POOL, wsem[s],
                    lambda: nc.gpsimd.dma_start(out=wt[s][:], in_=view[:, :, c0:c0 + 128]),
                    reads=list(tile_gate), writes=[wtB[s]])
                tstate["slot_users_left"][s] = -1
                tstate["issued"] += 1

        def acquire_tile(nusers):
            i = tstate["next_use"]
            tstate["next_use"] += 1
            assert i < tstate["issued"], "tile not issued"
            s = i % NWT
            tstate["slot_users_left"][s] = nusers
            return s

        def release_tile(s):
            tstate["slot_users_left"][s] -= 1
            if tstate["slot_users_left"][s] == 0:
                issue_tiles()

        c4 = new_dsem("c4")
        idB = Buf()
        ssB = Buf("ss")
        op(DVE, lambda: nc.vector.memset(ss1[:], 0.0), writes=[ssB])
        op(DVE, lambda: nc.vector.memset(epsc[:], EPS), writes=[ssB])
        op(ACT, lambda: nc.scalar.activation(rs1[:, 31:32], epsc[:], AF.Square), reads=[ssB], writes=[ssB])
        op(DVE, lambda: nc.vector.memset(onesb[:], 1.0), writes=[constB])
        op(DVE, lambda: nc.vector.memset(onesf[:], 1.0), writes=[constB])
        op(DVE, lambda: nc.vector.memset(st[:], 0.0), writes=[statsB])

        def load_consts():
            dma(SP, c4, lambda: nc.sync.dma_start(out=idf[:], in_=id_d), writes=[idB])
            dma(SP, csem, lambda: nc.sync.dma_start(out=cvec[:], in_=cvec_d.rearrange("p (j r) -> p j r", j=NJ)), writes=[constB])

        with ExitStack() as pc:
            g1 = sb(pc, "g1_s", [128, D], F32)
            g1B = Buf()
            c2 = new_dsem("c2")
            NXT = 6
            NHN = 4
            xt = [sb(pc, f"xt{i}", [128, D], F32) for i in range(NXT)]
            xtB = [Buf() for _ in range(NXT)]
            xsem = [new_dsem(f"xsem{i}") for i in range(NXT)]
            junk = sb(pc, "junk", [128, D], BF16)
            junkB = Buf()
            hn = [sb(pc, f"hn{i}", [128, D], BF16) for i in range(NHN)]
            hnB = [Buf() for _ in range(NHN)]

            def rows_of(c):
                return H if c == 0 else 128

            def stage_a_dma(c):
                s3 = c % NXT
                rows = rows_of(c)
                src = xh if c == 0 else xm[(c - 1) * 128:c * 128, :]
                dma(SP, xsem[s3], lambda: nc.sync.dma_start(out=xt[s3][0:rows, :], in_=src), writes=[xtB[s3]])

            def stage_a(c, with_dma=True):
                s3 = c % NXT
                sh = c % NHN
                rows = rows_of(c)
                if with_dma:
                    stage_a_dma(c)
                op(ACT, lambda: nc.scalar.activation(junk[0:rows, :], xt[s3][0:rows, :], AF.Square,
                                                     accum_out=ss1[0:rows, c:c + 1]),
                   reads=[xtB[s3], ssB], writes=[junkB])
                rb = Buf()
                op(ACT, lambda: nc.scalar.activation(rs1[0:rows, c:c + 1], ss1[0:rows, c:c + 1], AF.Sqrt, bias=epsc[0:rows, :], scale=1.0 / D),
                   reads=[junkB, ssB], writes=[rb])
                op(DVE, lambda: nc.vector.reciprocal(rs1[0:rows, c:c + 1], rs1[0:rows, c:c + 1]),
                   reads=[rb], writes=[rb])
                op(DVE, lambda: nc.vector.scalar_tensor_tensor(hn[sh][0:rows, :], xt[s3][0:rows, :], rs1[0:rows, c:c + 1],
                                                               g1[0:rows, :], ALU.mult, ALU.mult),
                   reads=[rb, xtB[s3], g1B], writes=[hnB[sh]])

            def stage_b(c):
                s3 = c % NHN
                rows = rows_of(c)
                tok0 = 0 if c == 0 else H + (c - 1) * 128
                ps, psB = next_slot()
                psb = ps[:].bitcast(BF16)

                def emit_tr():
                    last = None
                    for k in range(NJ):
                        last = nc.tensor.transpose(psb[:, k * 128:k * 128 + rows], hn[s3][0:rows, k * 128:(k + 1) * 128],
                                                   idb[0:rows, 0:rows])
                    return last
                op(PE, emit_tr, reads=[hnB[s3], constB], writes=[psB])
                src_ps = psb[:, 0:1024].rearrange("p (k t) -> p k t", k=NJ)[:, :, 0:rows]
                op(DVE, lambda: nc.vector.tensor_copy(hT[:, :, tok0:tok0 + rows], src_ps), reads=[psB], writes=[hTB])

            AHEAD = 3
            stage_a_dma(0)
            dma(SP, c2, lambda: nc.sync.dma_start(out=g1[:], in_=g1_d), writes=[g1B])
            load_consts()
            for c in range(1, AHEAD):
                stage_a_dma(c)
            tile_gate.append(g1B)
            issue_tiles()
            del tile_gate[:]
            stage_a(0, with_dma=False)
            op(DVE, lambda: nc.vector.tensor_copy(idb[:], idf[:]), reads=[idB], writes=[constB])
            for c in range(1, AHEAD):
                stage_a(c, with_dma=False)
            for c in range(17):
                if c + AHEAD < 17:
                    stage_a(c + AHEAD)
                stage_b(c)
            barrier()

        def main_unit(s, X, XB, tok0, extra_reads=()):
            ps, psB = next_slot()

            def emit():
                last = None
                for b in range(2):
                    for k in range(NJ):
                        last = nc.tensor.matmul(ps[:, b * 512:(b + 1) * 512], wt[s][:, k, :],
                                                X(k, tok0 + b * 512, 512), start=(k == 0), stop=(k == NJ - 1))
                return last
            op(PE, emit, reads=[wtB[s], XB] + list(extra_reads), writes=[psB])
            release_tile(s)
            return ps, psB

        def halo_unit(s):
            ps, psB = next_slot()

            def emit():
                last = None
                for k in range(NJ):
                    last = nc.tensor.matmul(ps[:, 0:H], wt[s][:, k, :], hT[:, k, 0:H], start=(k == 0), stop=(k == NJ - 1))
                return last
            op(PE, emit, reads=[wtB[s], hTB], writes=[psB])
            release_tile(s)
            return ps, psB

        def XhT(k, t0, n):
            return hT[:, k, H + t0:H + t0 + n]

        def Xva(k, t0, n):
            return UCb[:, k, t0:t0 + n]

        def Xvb(k, t0, n):
            return VB[:, k, t0:t0 + n]

        def conv_unit(dgt, dgB, taps, K, src, srcB, th):
            ps, psB = next_slot()

            def emit():
                last = None
                for b in range(2):
                    t0 = H + th * 1024 + b * 512 - (K - 1)
                    for i, k in enumerate(taps):
                        last = nc.tensor.matmul(ps[:, b * 512:(b + 1) * 512], dgt[:, i, :], src[:, t0 + k:t0 + k + 512],
                                                start=(i == 0), stop=(i == len(taps) - 1))
                return last
            op(PE, emit, reads=[dgB, srcB], writes=[psB])
            return ps, psB

        ucB = [Buf(f"uc{j}") for j in range(NJ)]
        vaB = Buf("va")
        vbB = Buf("vb")

        with ExitStack() as pc:
            NUA = 3
            ua = [sb(pc, f"ua{i}", [128, TT], BF16) for i in range(NUA)]
            uaB = [Buf(f"ua{i}") for i in range(NUA)]
            sg = [sb(pc, f"sg{i}", [128, 1024], F32) for i in range(2)]
            sgB = [Buf() for _ in range(2)]
            sgh = sb(pc, "sgh", [128, H], F32)
            sghB = Buf()
            hv = sb(pc, "hv", [128, H], F32)
            hvB = Buf()
            stt = sb(pc, "stt", [128, 16], F32)
            sttB = Buf()
            dg = sb(pc, "dg", [128, NPMAX, 128], BF16)
            dgB = Buf()
            ucb = sb(pc, "ucb", [128, 1024], BF16)
            ucq = sb(pc, "ucq", [128, 1024], BF16)
            ucbB, ucqB = Buf(), Buf()
            NEV = 3
            evt = [sb(pc, f"ev{i}", [128, 1024], F32) for i in range(NEV)]
            evB = [Buf() for _ in range(NEV)]
            evc = [0]

            def next_ev():
                e = evc[0] % NEV
                evc[0] += 1
                return evt[e], evB[e]
            bc = [sb(pc, f"bc{i}", [128, 1024], F32) for i in range(2)]
            bch = sb(pc, "bch", [128, H], F32)
            sbz = [sb(pc, f"sbz{i}", [128, 1024], F32) for i in range(2)]
            cxb = sb(pc, "cxb", [128, TT], BF16)
            dg3 = sb(pc, "dg3", [128, KB, 128], BF16)
            bcB = [Buf() for _ in range(2)]
            bchB = Buf()
            sbzB = [Buf() for _ in range(2)]
            cxbB = Buf()
            dg3B = Buf()

            dve_bg = []
            pool_bg = []

            credit = [0.0, 0.0]
            lag_q = [[], []]

            def defer(fn):
                lag_q[1].append(fn)

            timed = []
            stats_ready = []

            def after_pumps(n, fn):
                timed.append([n, fn])

            def pump(force=False, rate=None):
                for t_ in timed:
                    t_[0] -= 1
                due = [t_ for t_ in timed if t_[0] <= 0]
                for t_ in due:
                    timed.remove(t_)
                    t_[1]()
                for fn in lag_q[0]:
                    fn()
                lag_q[0] = lag_q[1]
                lag_q[1] = []
                credit[0] += DVE_RATE if rate is None else rate
                credit[1] += POOL_RATE
                while dve_bg and (force or credit[0] >= 1.0):
                    credit[0] -= 1.0
                    dve_bg.pop(0)()
                    if force:
                        break
                while pool_bg and (force or credit[1] >= 1.0):
                    credit[1] -= 1.0
                    pool_bg.pop(0)()
                    if force:
                        break

            def tap_op(eng, e, j, k):
                u = ua[j % NUA]
                uBf = uaB[j % NUA]
                src = u[:, H - (KA - 1) + k:H - (KA - 1) + k + T]
                op(eng, lambda: e.scalar_tensor_tensor(UC[:, j, :], src, cvec[:, j, k:k + 1], UC[:, j, :], ALU.mult, ALU.add),
                   reads=[uBf, constB, ucB[j]], writes=[ucB[j]])

            def pool_tap(j, k, th):
                u = ua[j % NUA]
                uBf = uaB[j % NUA]
                o = H - (KA - 1) + k + th * 1024
                op(POOL, lambda: nc.gpsimd.tensor_scalar(tmpP[:], u[:, o:o + 1024], cvec[:, j, k:k + 1], None, ALU.mult),
                   reads=[uBf, constB], writes=[tmpPB])
                op(POOL, lambda: nc.gpsimd.tensor_tensor(UC[:, j, th * 1024:(th + 1) * 1024], UC[:, j, th * 1024:(th + 1) * 1024], tmpP[:], ALU.add),
                   reads=[tmpPB, ucB[j]], writes=[ucB[j]])

            def sched_conv(j):
                nP, nQ = NP_TAPS[j], NQ_TAPS[j]
                dtaps = list(range(nP, KA - nQ))
                qtaps = list(range(KA - nQ, KA))

                def enqueue_pool():
                    for k in qtaps:
                        for th in range(2):
                            pool_bg.append(lambda k=k, th=th: pool_tap(j, k, th))
                for idx, k in enumerate(dtaps):
                    dve_bg.append(lambda k=k: tap_op(DVE, nc.vector, j, k))
                if qtaps:
                    dve_bg.append(enqueue_pool)
                dve_bg.append(lambda: stats_ready.append(j))

            def stats_act(j, th):
                tsl = slice(th * 1024, (th + 1) * 1024)
                op(ACT, lambda: nc.scalar.activation(ucq[:], UC[:, j, tsl], AF.Square), reads=[ucB[j]], writes=[ucqB])
                op(ACT, lambda: nc.scalar.copy(ucb[:], UC[:, j, tsl]), reads=[ucB[j]], writes=[ucbB])

            def stats_pe(j, th):
                ps, psB = next_slot()

                def emit_st():
                    last = None
                    for c in range(8):
                        nc.tensor.matmul(ps[:, 2 * c:2 * c + 1], ucb[:, c * 128:(c + 1) * 128], onesb[:, 0:1], start=True, stop=True)
                        last = nc.tensor.matmul(ps[:, 2 * c + 1:2 * c + 2], ucq[:, c * 128:(c + 1) * 128], onesb[:, 0:1], start=True, stop=True)
                    return last
                op(PE, emit_st, reads=[ucbB, ucqB, constB], writes=[psB])
                op(ACT, lambda: nc.scalar.copy(stt[:], ps[:, 0:16]), reads=[psB], writes=[sttB])
                op(DVE, lambda: nc.vector.tensor_tensor(st[:, th * 16:(th + 1) * 16], stt[:], st[:, th * 16:(th + 1) * 16], ALU.add),
                   reads=[sttB, statsB], writes=[statsB])

            def sched_stats(j):
                def mid():
                    stats_pe(j, 0)
                    stats_act(j, 1)
                after_pumps(4, lambda: stats_act(j, 0))
                after_pumps(8, mid)
                after_pumps(12, lambda: stats_pe(j, 1))

            def emit_A1(j):
                u = ua[j % NUA]
                uBf = uaB[j % NUA]
                nP = NP_TAPS[j]
                sG = acquire_tile(3)
                sV = acquire_tile(3)
                ps, psB = halo_unit(sG)
                op(ACT, lambda: nc.scalar.activation(sgh[:], ps[:, 0:H], AF.Sigmoid), reads=[psB], writes=[sghB])
                ps, psB = halo_unit(sV)
                op(ACT, lambda: nc.scalar.copy(hv[:], ps[:, 0:H]), reads=[psB], writes=[hvB])
                op(DVE, lambda: nc.vector.tensor_tensor(u[:, 0:H], hv[:], sgh[:], ALU.mult),
                   reads=[hvB, sghB], writes=[uBf])
                for th in range(2):
                    ps, psB = main_unit(sG, XhT, hTB, th * 1024)
                    op(ACT, lambda: nc.scalar.activation(sg[th][:], ps[:], AF.Sigmoid), reads=[psB], writes=[sgB[th]])
                    pump(rate=R_N)
                    ps, psB = main_unit(sV, XhT, hTB, th * 1024)
                    ev, eB = next_ev()
                    op(ACT, lambda: nc.scalar.copy(ev[:], ps[:]), reads=[psB], writes=[eB])
                    defer(lambda ev=ev, eB=eB, th=th: op(
                        DVE, lambda: nc.vector.tensor_tensor(u[:, H + th * 1024:H + (th + 1) * 1024], ev[:], sg[th][:], ALU.mult),
                        reads=[eB, sgB[th]], writes=[uBf]))
                    pump(rate=R_M)

            def build_dg(j):
                nP = NP_TAPS[j]
                in0 = bass.AP(idb, 0, [[128, 128], [0, nP], [1, 128]])
                in1 = bass.AP(cvec, j * NCV, [[NJ * NCV, 128], [1, nP], [0, 128]])
                op(DVE, lambda: nc.vector.tensor_tensor(dg[:, 0:nP, :], in0, in1, ALU.mult), reads=[constB], writes=[dgB])

            def emit_A1conv(j):
                u = ua[j % NUA]
                uBf = uaB[j % NUA]
                nP = NP_TAPS[j]
                bias = cvec[:, j, 31:32]
                for th in range(2):
                    ps, psB = conv_unit(dg, dgB, list(range(nP)), KA, u, uBf, th)
                    op(ACT, lambda: nc.scalar.activation(UC[:, j, th * 1024:(th + 1) * 1024], ps[:], AF.Identity, bias=bias),
                       reads=[psB, constB], writes=[ucB[j]])
                    pump(rate=R_C)
                sched_conv(j)

            def emit_B(j, mid_hook=None):
                if stats_ready and not timed:
                    sched_stats(stats_ready.pop(0))
                in0 = bass.AP(idb, 0, [[128, 128], [0, KB], [1, 128]])
                in1 = bass.AP(cvec, j * NCV + 35, [[NJ * NCV, 128], [1, KB], [0, 128]])
                op(DVE, lambda: nc.vector.tensor_tensor(dg3[:], in0, in1, ALU.mult), reads=[constB], writes=[dg3B])
                sC = acquire_tile(3)
                sX = acquire_tile(3)
                sBt = acquire_tile(2)
                sZ = acquire_tile(2)
                ps, psB = halo_unit(sC)
                op(ACT, lambda: nc.scalar.copy(bch[:], ps[:, 0:H]), reads=[psB], writes=[bchB])
                ps, psB = halo_unit(sX)
                op(ACT, lambda: nc.scalar.copy(hv[:], ps[:, 0:H]), reads=[psB], writes=[hvB])
                op(DVE, lambda: nc.vector.tensor_tensor(cxb[:, 0:H], hv[:], bch[:], ALU.mult),
                   reads=[hvB, bchB], writes=[cxbB])
                for th in range(2):
                    ps, psB = main_unit(sC, XhT, hTB, th * 1024)
                    op(ACT, lambda: nc.scalar.copy(bc[th][:], ps[:]), reads=[psB], writes=[bcB[th]])
                    pump(rate=R_N)
                    ps, psB = main_unit(sX, XhT, hTB, th * 1024)
                    ev, eB = next_ev()
                    op(ACT, lambda: nc.scalar.copy(ev[:], ps[:]), reads=[psB], writes=[eB])
                    op(DVE, lambda: nc.vector.tensor_tensor(cxb[:, H + th * 1024:H + (th + 1) * 1024], ev[:], bc[th][:], ALU.mult),
                       reads=[eB, bcB[th]], writes=[cxbB])
                    pump(rate=R_M)
                if mid_hook is not None:
                    mid_hook()
                for th in range(2):
                    ps, psB = main_unit(sBt, XhT, hTB, th * 1024)
                    ev, eB = next_ev()
                    op(ACT, lambda: nc.scalar.copy(ev[:], ps[:]), reads=[psB], writes=[eB])
                    pump(rate=R_N)
                    ps, psB = main_unit(sZ, XhT, hTB, th * 1024)
                    op(ACT, lambda: nc.scalar.activation(sbz[th][:], ps[:], AF.Silu), reads=[psB], writes=[sbzB[th]])
                    defer(lambda ev=ev, eB=eB, th=th: op(
                        DVE, lambda: nc.vector.tensor_tensor(sbz[th][:], ev[:], sbz[th][:], ALU.mult),
                        reads=[eB, sbzB[th]], writes=[sbzB[th]]))
                    pump(rate=R_M)
                for th in range(2):
                    ps, psB = conv_unit(dg3, dg3B, list(range(KB)), KB, cxb, cxbB, th)
                    ev, eB = next_ev()
                    op(ACT, lambda: nc.scalar.copy(ev[:], ps[:]), reads=[psB], writes=[eB])
                    defer(lambda ev=ev, eB=eB, th=th: op(
                        DVE, lambda: nc.vector.tensor_tensor(VB[:, j, th * 1024:(th + 1) * 1024], ev[:], sbz[th][:], ALU.mult),
                        reads=[eB, sbzB[th]], writes=[vbB]))
                    pump(rate=R_3)

            for step in range(NJ + LEAD):
                if stats_ready and not timed:
                    sched_stats(stats_ready.pop(0))
                if step < NJ:
                    emit_A1(step)
                    if step == LEAD:
                        for j0 in range(LEAD):
                            build_dg(j0)
                            emit_A1conv(j0)
                if step >= LEAD:
                    if step < NJ:
                        build_dg(step)
                    emit_B(step - LEAD, mid_hook=(lambda st_=step: emit_A1conv(st_)) if step < NJ else None)
            while dve_bg or pool_bg or lag_q[0] or lag_q[1] or timed:
                pump(force=True)
            while stats_ready:
                j_ = stats_ready.pop(0)
                for th_ in range(2):
                    stats_act(j_, th_)
                    stats_pe(j_, th_)
            barrier()

        with ExitStack() as pc:
            rbc = sb(pc, "rbc", [128, T], F32)
            nbc = sb(pc, "nbc", [128, T], F32)
            rbcB, nbcB = Buf(), Buf()
            zt = [sb(pc, f"zt{i}", [128, 1024], F32) for i in range(2)]
            sz = [sb(pc, f"sz{i}", [128, 1024], F32) for i in range(2)]
            ztB = [Buf() for _ in range(2)]
            szB = [Buf() for _ in range(2)]
            rhsR = sb(pc, "rhsR", [128, T], F32)
            rhsN = sb(pc, "rhsN", [128, T], F32)
            units = [(j, th) for j in range(NJ) for th in range(2)]
            tile_of = {}

            done1a = set()

            def a2_stage1a(u):
                if u in done1a:
                    return
                done1a.add(u)
                j, th = units[u]
                q = u % 2
                if th == 0:
                    tile_of[j] = acquire_tile(2)
                ps, psB = main_unit(tile_of[j], XhT, hTB, th * 1024)
                op(ACT, lambda: nc.scalar.activation(sz[q][:], ps[:], AF.Silu), reads=[psB], writes=[szB[q]])

            def a2_stage1(u):
                a2_stage1a(u)
                j, th = units[u]
                q = u % 2
                tsl = slice(th * 1024, (th + 1) * 1024)
                op(DVE, lambda: nc.vector.tensor_tensor(zt[q][:], UC[:, j, tsl], rbc[:, tsl], ALU.mult),
                   reads=[ucB[j], rbcB], writes=[ztB[q]])
                op(DVE, lambda: nc.vector.tensor_tensor(zt[q][:], zt[q][:], nbc[:, tsl], ALU.add),
                   reads=[nbcB, ztB[q]], writes=[ztB[q]])
                op(ACT, lambda: nc.scalar.activation(zt[q][:], zt[q][:], AF.Silu, bias=cvec[:, j, 33:34], scale=cvec[:, j, 32:33]),
                   reads=[ztB[q], constB], writes=[ztB[q]])

            def a2_stage2(u):
                j, th = units[u]
                q = u % 2
                op(DVE, lambda: nc.vector.tensor_tensor(UCb[:, j, th * 1024:(th + 1) * 1024], zt[q][:], sz[q][:], ALU.mult),
                   reads=[ztB[q], szB[q], ucB[j]], writes=[vaB, ucB[j]])

            stB = statsB
            st3 = st[:].rearrange("p (c q) -> p c q", q=2)
            mu, msq, var, rstd, nmr = (stw[:, i, :] for i in range(5))
            op(DVE, lambda: nc.vector.tensor_scalar(mu, st3[:, :, 0], 1.0 / D, None, ALU.mult), reads=[stB], writes=[stB])
            op(DVE, lambda: nc.vector.tensor_tensor(msq, mu, mu, ALU.mult), reads=[stB], writes=[stB])
            op(DVE, lambda: nc.vector.scalar_tensor_tensor(var, st3[:, :, 1], 1.0 / D, msq, ALU.mult, ALU.subtract), reads=[stB], writes=[stB])
            op(ACT, lambda: nc.scalar.activation(rstd, var, AF.Sqrt, bias=epsc[:], scale=1.0), reads=[stB, constB], writes=[stB])
            op(DVE, lambda: nc.vector.reciprocal(rstd, rstd), reads=[stB], writes=[stB])
            op(DVE, lambda: nc.vector.scalar_tensor_tensor(nmr, mu, -1.0, rstd, ALU.mult, ALU.mult), reads=[stB], writes=[stB])
            rB, nB = Buf(), Buf()
            idbc = bass.AP(idf, 0, [[128, 128], [0, 16], [1, 128]])
            op(DVE, lambda: nc.vector.tensor_tensor(rhsR[:].rearrange("p (c t) -> p c t", c=16), idbc,
                                                    bass.AP(stw, 3 * 16, [[96, 128], [1, 16], [0, 128]]), ALU.mult),
               reads=[stB, idB], writes=[rB])
            op(DVE, lambda: nc.vector.tensor_tensor(rhsN[:].rearrange("p (c t) -> p c t", c=16), idbc,
                                                    bass.AP(stw, 4 * 16, [[96, 128], [1, 16], [0, 128]]), ALU.mult),
               reads=[stB, idB], writes=[nB])
            a2_stage1a(0)
            a2_stage1a(1)
            for (srcT, srcB, dst, dstB) in ((rhsR, rB, rbc, rbcB), (rhsN, nB, nbc, nbcB)):
                for th in range(2):
                    ps, psB = next_slot()

                    def emit_bc():
                        last = None
                        for b in range(2):
                            last = nc.tensor.matmul(ps[:, b * 512:(b + 1) * 512], onesf[:], srcT[:, th * 1024 + b * 512:th * 1024 + (b + 1) * 512],
                                                    start=True, stop=True)
                        return last
                    op(PE, emit_bc, reads=[srcB, constB], writes=[psB])
                    op(ACT, lambda: nc.scalar.copy(dst[:, th * 1024:(th + 1) * 1024], ps[:]), reads=[psB], writes=[dstB])
            a2_stage1(0)
            for u in range(len(units)):
                if u + 1 < len(units):
                    a2_stage1(u + 1)
                a2_stage2(u)

            mB = Buf("m")
            wo = sb(pc, "wo", [128, NJ, D], BF16)
            woB = Buf()
            wosem = new_dsem("wosem")
            if True:
                sga = [zt[0][:], zt[1][:]]
                sgaB = ztB
                sgb = [sz[0][:], sz[1][:]]
                sgbB = szB
                t1 = [rhsR[:, 0:1024], rhsR[:, 1024:2048]]
                t1B = [Buf() for _ in range(2)]
                for b_ in t1B:
                    b_.w = rB.w
                    b_.r = dict(rB.r)
                mBs = [Buf("m0"), Buf("m1")]

                NXR = 4
                g2 = rhsR[:, 1024:2048]
                g2B = Buf()
                c3 = new_dsem("c3")
                xr = [rbc[:, 0:1024], rbc[:, 1024:2048], nbc[:, 0:1024], nbc[:, 1024:2048]]
                xrB = [Buf() for _ in range(NXR)]
                for b_, src_ in zip(xrB, (rbcB, rbcB, nbcB, nbcB)):
                    b_.w, b_.r = src_.w, dict(src_.r)
                xrsem = [new_dsem(f"xrsem{i}") for i in range(NXR)]
                yo = [rhsN[:, 0:1024], rhsN[:, 1024:2048]]
                yoB = [Buf() for _ in range(2)]
                for b_ in yoB:
                    b_.w, b_.r = nB.w, dict(nB.r)
                yosem = [new_dsem(f"yosem{i}") for i in range(2)]

                def load_xr(t_):
                    s_ = t_ % NXR
                    dma(SP, xrsem[s_], lambda: nc.sync.dma_start(out=xr[s_], in_=xm[t_ * 128:(t_ + 1) * 128, :]), writes=[xrB[s_]])

                def u_ga(i, th, q, sGa):
                    ps, psB = main_unit(sGa, XhT, hTB, th * 1024)
                    op(ACT, lambda: nc.scalar.activation(sga[q], ps[:], AF.Sigmoid), reads=[psB], writes=[sgaB[q]])

                def u_ya(i, th, q, sWa):
                    ps, psB = main_unit(sWa, Xva, vaB, th * 1024)
                    op(DVE, lambda: nc.vector.scalar_tensor_tensor(t1[q], ps[:], cvec[:, i, 34:35], sga[q], ALU.add, ALU.mult),
                       reads=[psB, sgaB[q], constB], writes=[t1B[q]])

                def u_gb(i, th, q, sGb):
                    ps, psB = main_unit(sGb, XhT, hTB, th * 1024)
                    op(ACT, lambda: nc.scalar.activation(sgb[q], ps[:], AF.Sigmoid), reads=[psB], writes=[sgbB[q]])

                def u_yb(i, th, q, sWb):
                    ps, psB = main_unit(sWb, Xvb, vbB, th * 1024)
                    op(DVE, lambda: nc.vector.tensor_tensor(sgb[q], ps[:], sgb[q], ALU.mult),
                       reads=[psB, sgbB[q]], writes=[sgbB[q]])

                def u_m(i, th, q):
                    op(DVE, lambda: nc.vector.tensor_tensor(UCb[:, i, T + th * 1024:T + (th + 1) * 1024], t1[q], sgb[q], ALU.add),
                       reads=[t1B[q], sgbB[q]], writes=[mBs[th]])

                for i in range(NJ):
                    sGa = acquire_tile(2)
                    sWa = acquire_tile(2)
                    sGb = acquire_tile(2)
                    sWb = acquire_tile(2)
                    if i == NJ - 1:
                        dma(POOL, wosem, lambda: nc.gpsimd.dma_start(out=wo[:], in_=w_o_v), writes=[woB])
                        for t_ in range(NXR):
                            load_xr(t_)
                    if i == 0:
                        for th in range(2):
                            u_ga(i, th, th, sGa)
                        for th in range(2):
                            u_gb(i, th, th, sGb)
                        for th in range(2):
                            u_yb(i, th, th, sWb)
                        for th in range(2):
                            u_ya(i, th, th, sWa)
                            u_m(i, th, th)
                    else:
                        for th in range(2):
                            q = th
                            u_ga(i, th, q, sGa)
                            u_ya(i, th, q, sWa)
                            u_gb(i, th, q, sGb)
                            u_yb(i, th, q, sWb)
                            u_m(i, th, q)
            g2B.w, g2B.r = t1B[1].w, dict(t1B[1].r)
            dma(SP, c3, lambda: nc.sync.dma_start(out=g2, in_=g2_d), writes=[g2B])
            junk2 = rhsR[:].bitcast(BF16)[:, 0:D]
            junk2B = Buf()
            junk2B.w, junk2B.r = t1B[0].w, dict(t1B[0].r)
            ss2 = ss1[:, 0:16]
            rs2 = rs1[:, 0:16]
            ss2B = Buf()
            op(DVE, lambda: nc.vector.memset(ss2, 0.0), writes=[ss2B])
            rbs = [Buf() for _ in range(16)]

            def fin_stage1(tc):
                s3 = tc % NXR
                ps, psB = next_slot()

                def emit_f():
                    last = None
                    for n in range(2):
                        for k in range(NJ):
                            last = nc.tensor.matmul(ps[:, n * 512:(n + 1) * 512], UCb[:, k, T + tc * 128:T + (tc + 1) * 128],
                                                    wo[:, k, n * 512:(n + 1) * 512], start=(k == 0), stop=(k == NJ - 1))
                    return last
                op(PE, emit_f, reads=[mBs[tc // 8], woB], writes=[psB])
                op(DVE, lambda: nc.vector.tensor_tensor(xr[s3], ps[:], xr[s3], ALU.add), reads=[psB, xrB[s3]], writes=[xrB[s3]])
                op(ACT, lambda: nc.scalar.activation(junk2, xr[s3], AF.Square, accum_out=ss2[:, tc:tc + 1]),
                   reads=[xrB[s3], ss2B], writes=[junk2B])
                op(ACT, lambda: nc.scalar.activation(rs2[:, tc:tc + 1], ss2[:, tc:tc + 1], AF.Sqrt, bias=epsc[:], scale=1.0 / D),
                   reads=[junk2B, constB], writes=[rbs[tc]])

            def fin_stage2(tc):
                s3 = tc % NXR
                s2 = tc % 2
                rb = rbs[tc]
                op(DVE, lambda: nc.vector.reciprocal(rs2[:, tc:tc + 1], rs2[:, tc:tc + 1]),
                   reads=[rb], writes=[rb])
                op(DVE, lambda: nc.vector.scalar_tensor_tensor(yo[s2], xr[s3], rs2[:, tc:tc + 1], g2, ALU.mult, ALU.mult),
                   reads=[rb, xrB[s3], g2B], writes=[yoB[s2]])
                dma(SP, yosem[s2], lambda: nc.sync.dma_start(out=y[tc * 128:(tc + 1) * 128, :], in_=yo[s2]), reads=[yoB[s2]])
                if tc + NXR < 16:
                    load_xr(tc + NXR)

            fin_stage1(0)
            for tc in range(16):
                if tc + 1 < 16:
                    fin_stage1(tc + 1)
                fin_stage2(tc)
            barrier()
    return nc


_NC_CACHE = {}


def kernel(x, meta_tokens, norm_g, w_in, conv_a_w, conv_a_b, ln_a_g, ln_a_b,
           w_a_out, b_a_out, conv_b_w, w_b_out, w_out, final_g):
    f = np.float32
    x = np.asarray(x, f)
    meta = np.asarray(meta_tokens, f)
    B = x.shape[0]
    ncores = 8
    pk = np.concatenate([np.asarray(conv_a_w, f)[0], np.asarray(conv_a_b, f), np.asarray(ln_a_g, f),
                         np.asarray(ln_a_b, f), np.asarray(b_a_out, f), np.asarray(conv_b_w, f)[0]], axis=0)
    assert pk.shape == (NCV, D)
    cvec = np.ascontiguousarray(pk.reshape(NCV, NJ, 128).transpose(2, 1, 0)).reshape(128, NJ * NCV)
    g1 = np.ascontiguousarray(np.broadcast_to(np.asarray(norm_g, f).reshape(1, D), (128, D)))
    g2 = np.ascontiguousarray(np.broadcast_to(np.asarray(final_g, f).reshape(1, D), (128, D)))
    ident = np.eye(128, dtype=f)
    wi = np.ascontiguousarray(np.asarray(w_in, f)[0])
    wa = np.ascontiguousarray(np.asarray(w_a_out, f)[0])
    wb = np.ascontiguousarray(np.asarray(w_b_out, f)[0])
    wo = np.ascontiguousarray(np.asarray(w_out, f)[0])
    in_maps = []
    for i in range(ncores):
        b, hf = i // 2, i % 2
        xmi = np.ascontiguousarray(x[b, hf * T:(hf + 1) * T])
        if hf == 0:
            xhi = np.concatenate([np.zeros((H - 16, D), f), meta], axis=0)
        else:
            xhi = np.ascontiguousarray(x[b, T - H:T])
        in_maps.append({"xm": xmi, "xh": xhi, "w_in": wi, "w_a": wa, "w_b": wb, "w_o": wo,
                        "cvec": cvec, "g1": g1, "g2": g2, "ident": ident})
    if "nc" not in _NC_CACHE:
        _NC_CACHE["nc"] = build_nc()
    nc = _NC_CACHE["nc"]
    res = run_bass_kernel_spmd(nc, in_maps, core_ids=list(range(ncores)))
    out = np.empty((B, 2 * T, D), f)
    for i in range(ncores):
        b, hf = i // 2, i % 2
        out[b, hf * T:(hf + 1) * T] = res.results[i]["y"]
    return out
```
